# Optimizing a Trainium2 kernel written in Bass

```python
import jax, jax.numpy as jnp
from jax import lax
import numpy as np

D_MODEL = 1024
BATCH = 8
SEQ = 2048
DEPTH = 2
DEC_BATCH = 1
DEC_SEQ = 16384
PAST_LEN = 128

CHUNK = 128
D_A = D_MODEL // 2
A_HEAD_DIM = 128
A_HEADS = D_A // A_HEAD_DIM
D_B = D_MODEL // 2
B_GROUP_DIM = 128
B_GROUPS = D_B // B_GROUP_DIM
D_PROJ_AB = 2 * D_A + D_B
D_MIX_AB = D_A + D_B
D_C = D_MODEL
CONV_W = 3
D_FF = ((8 * D_MODEL // 3 + 127) // 128) * 128
N_EVEN = (DEPTH + 1) // 2
N_ODD = DEPTH // 2
EPS = 1e-6

kernel_name = "hybrid_gmlp_fnet_shortconv_encoder"


def rmsnorm(x, g):
    xf = x.astype(jnp.float32)
    y = xf * lax.rsqrt(jnp.mean(xf * xf, axis=-1, keepdims=True) + EPS)
    return (y * g.astype(jnp.float32)).astype(x.dtype)


def dwconv3(x, w, b):
    xp = jnp.pad(x, ((0, 0), (1, 1), (0, 0)))
    return xp[:, :-2] * w[0] + xp[:, 1:-1] * w[1] + xp[:, 2:] * w[2] + b


def mixer_ab(h, w_in, sgu_gain, w_s, b_s, w_out):
    Bn, S, _ = h.shape
    p = h @ w_in
    u = jax.nn.gelu(p[..., :D_A])
    v = jax.nn.gelu(p[..., D_A:2 * D_A])
    f = p[..., 2 * D_A:]
    v = rmsnorm(v.reshape(Bn, S, A_HEADS, A_HEAD_DIM), sgu_gain)
    vc = v.reshape(Bn, S // CHUNK, CHUNK, A_HEADS, A_HEAD_DIM)
    s = jnp.einsum('hpq,bnqhd->bnphd', w_s, vc) + jnp.transpose(b_s)[None, None, :, :, None]
    a = u * s.reshape(Bn, S, D_A)
    fr = f.reshape(Bn, S, B_GROUPS, B_GROUP_DIM).astype(jnp.float32)
    ff = jnp.fft.fftn(fr, axes=(1, 3), norm='ortho').real.astype(h.dtype).reshape(Bn, S, D_B)
    return jnp.concatenate([a, ff], axis=-1) @ w_out


def mixer_c(h, w_in, conv_w, conv_b, w_out):
    p = h @ w_in
    gate_b = p[..., :D_C]
    gate_c = p[..., D_C:2 * D_C]
    z = p[..., 2 * D_C:]
    y = gate_b * dwconv3(gate_c * z, conv_w, conv_b)
    return y @ w_out


def conv_ffn(h, w_up, conv_w, conv_b, w_down):
    hu = dwconv3(h @ w_up, conv_w, conv_b)
    g = hu[..., :D_FF]
    val = hu[..., D_FF:]
    return (jax.nn.silu(g) * val) @ w_down


def trunk(x, norm_mix, w_in_ab, sgu_gain, w_s, b_s, w_out_ab,
          w_in_c, conv_w_c, conv_b_c, w_out_c,
          norm_ffn, w_up, ffn_conv_w, ffn_conv_b, w_down, final_norm):
    for l in range(DEPTH):
        h = rmsnorm(x, norm_mix[l])
        if l % 2 == 0:
            i = l // 2
            x = x + mixer_ab(h, w_in_ab[i], sgu_gain[i], w_s[i], b_s[i], w_out_ab[i])
        else:
            i = l // 2
            x = x + mixer_c(h, w_in_c[i], conv_w_c[i], conv_b_c[i], w_out_c[i])
        h = rmsnorm(x, norm_ffn[l])
        x = x + conv_ffn(h, w_up[l], ffn_conv_w[l], ffn_conv_b[l], w_down[l])
    return rmsnorm(x, final_norm)


def setup_inputs(seed: int = 0) -> dict:
    key = jax.random.key(seed)
    ks = jax.random.split(key, 20)
    f32 = jnp.float32
    nrm = lambda k, shape, scale: jax.random.normal(k, shape, f32) * scale
    return {
        "x_prompt": jax.random.normal(ks[0], (BATCH, SEQ, D_MODEL), f32),
        "x_sample": jax.random.normal(ks[1], (DEC_BATCH, DEC_SEQ, D_MODEL), f32),
        "norm_mix": 1.0 + nrm(ks[2], (DEPTH, D_MODEL), 0.02),
        "w_in_ab": nrm(ks[3], (N_EVEN, D_MODEL, D_PROJ_AB), D_MODEL ** -0.5),
        "sgu_gain": 1.0 + nrm(ks[4], (N_EVEN, A_HEADS, A_HEAD_DIM), 0.02),
        "w_s": nrm(ks[5], (N_EVEN, A_HEADS, CHUNK, CHUNK), CHUNK ** -0.5),
        "b_s": 1.0 + nrm(ks[6], (N_EVEN, A_HEADS, CHUNK), 0.02),
        "w_out_ab": nrm(ks[7], (N_EVEN, D_MIX_AB, D_MODEL), D_MIX_AB ** -0.5),
        "w_in_c": nrm(ks[8], (N_ODD, D_MODEL, 3 * D_C), D_MODEL ** -0.5),
        "conv_w_c": nrm(ks[9], (N_ODD, CONV_W, D_C), CONV_W ** -0.5),
        "conv_b_c": nrm(ks[10], (N_ODD, D_C), 0.02),
        "w_out_c": nrm(ks[11], (N_ODD, D_C, D_MODEL), D_C ** -0.5),
        "norm_ffn": 1.0 + nrm(ks[12], (DEPTH, D_MODEL), 0.02),
        "w_up": nrm(ks[13], (DEPTH, D_MODEL, 2 * D_FF), D_MODEL ** -0.5),
        "ffn_conv_w": nrm(ks[14], (DEPTH, CONV_W, 2 * D_FF), CONV_W ** -0.5),
        "ffn_conv_b": nrm(ks[15], (DEPTH, 2 * D_FF), 0.02),
        "w_down": nrm(ks[16], (DEPTH, D_FF, D_MODEL), D_FF ** -0.5),
        "final_norm": 1.0 + nrm(ks[17], (D_MODEL,), 0.02),
    }


def reference(x_prompt, x_sample, norm_mix, w_in_ab, sgu_gain, w_s, b_s, w_out_ab,
              w_in_c, conv_w_c, conv_b_c, w_out_c,
              norm_ffn, w_up, ffn_conv_w, ffn_conv_b, w_down, final_norm):
    y_prompt = trunk(x_prompt, norm_mix, w_in_ab, sgu_gain, w_s, b_s, w_out_ab,
                     w_in_c, conv_w_c, conv_b_c, w_out_c,
                     norm_ffn, w_up, ffn_conv_w, ffn_conv_b, w_down, final_norm)
    y_sample = trunk(x_sample, norm_mix, w_in_ab, sgu_gain, w_s, b_s, w_out_ab,
                     w_in_c, conv_w_c, conv_b_c, w_out_c,
                     norm_ffn, w_up, ffn_conv_w, ffn_conv_b, w_down, final_norm)
    return (y_prompt, y_sample)
```

```python
import numpy as np
import ml_dtypes
import concourse.bass as bass
import concourse.mybir as mybir
from concourse.bass_utils import run_bass_kernel_spmd

F32 = mybir.dt.float32
BF16 = mybir.dt.bfloat16
I32 = mybir.dt.int32
AF = mybir.ActivationFunctionType
ALU = mybir.AluOpType
NPBF = ml_dtypes.bfloat16

D = 1024
S_P = 2048
S_S = 16384
NCORE = 8
W = 2054
HALO = 3
E0 = 125
DFF = 2816
NFC = 22
EPS = 1e-6
TT = [(0, 512), (512, 512), (1024, 512), (1536, 512), (2048, 6)]
CT = [(510 * n, 512, 510) for n in range(4)] + [(2040, 16, 14)]
GROUPS = [list(range(0, 6)), list(range(6, 12)), list(range(12, 17)), list(range(17, 22))]


class Op:
    __slots__ = ("eng", "fn", "deps", "sig", "sigval", "kind", "key", "dval")


class Sched:
    ENGS = ["sp", "act", "dve", "pool", "pe"]

    def __init__(self):
        self.ops = {e: [] for e in self.ENGS}
        self.lastw = {}
        self.readers = {}
        self.bar = []
        self.keycnt = {}
        self.lastdma = {}

    def _mk(self, eng, fn, R, Wr):
        o = Op()
        o.eng, o.fn, o.sig, o.sigval, o.kind, o.key, o.dval = eng, fn, False, 0, "c", None, 0
        deps = set(self.bar)
        for r in R:
            w = self.lastw.get(r)
            if w is not None:
                deps.add(w)
        for r in Wr:
            w = self.lastw.get(r)
            if w is not None:
                deps.add(w)
            for rd in self.readers.get(r, ()):
                deps.add(rd)
        for r in R:
            self.readers.setdefault(r, []).append(o)
        for r in Wr:
            self.lastw[r] = o
            self.readers[r] = []
        deps.discard(o)
        o.deps = deps
        for d in deps:
            if d.kind == "c":
                d.sig = True
        self.ops[eng].append(o)
        return o

    def op(self, eng, fn, R=(), Wr=()):
        return self._mk(eng, fn, R, Wr)

    def dma(self, eng, fn, R, Wr, key):
        o = self._mk(eng, fn, R, Wr)
        o.kind = "d"
        o.key = key
        self.keycnt[key] = self.keycnt.get(key, 0) + 1
        o.dval = 16 * self.keycnt[key]
        self.lastdma[key] = o
        return o

    def barrier(self):
        b = []
        for e in self.ENGS:
            for o in reversed(self.ops[e]):
                if o.kind == "c":
                    o.sig = True
                    b.append(o)
                    break
        b.extend(self.lastdma.values())
        self.bar = b

    def emit(self, nc):
        sems = {e: nc.alloc_semaphore("s_" + e) for e in self.ENGS}
        dsem = {k: nc.alloc_semaphore("d_%d" % i) for i, k in enumerate(self.keycnt)}
        for e in self.ENGS:
            cnt = 0
            for o in self.ops[e]:
                if o.kind == "c" and o.sig:
                    cnt += 1
                    o.sigval = cnt
        fin = []
        for e in self.ENGS:
            for o in reversed(self.ops[e]):
                if o.kind == "c" and o.sig:
                    fin.append((sems[e], o.sigval))
                    break
        for k, o in self.lastdma.items():
            fin.append((dsem[k], o.dval))
        ops = self.ops

        def make(ename):
            def body(eng):
                waited = {}
                for o in ops[ename]:
                    need = {}
                    for d in o.deps:
                        if d.kind == "c":
                            if d.eng == "pe" and ename == "pe":
                                continue
                            sem, val = sems[d.eng], d.sigval
                        else:
                            sem, val = dsem[d.key], d.dval
                        if waited.get(id(sem), 0) >= val:
                            continue
                        if need.get(id(sem), (None, 0))[1] < val:
                            need[id(sem)] = (sem, val)
                    for sid, (sem, val) in need.items():
                        eng.wait_ge(sem, val)
                        waited[sid] = val
                    ins = o.fn(eng)
                    if o.kind == "c":
                        if o.sig:
                            ins.then_inc(sems[ename], 1)
                    else:
                        ins.then_inc(dsem[o.key], 16)
                if ename == "sp":
                    for sem, val in fin:
                        eng.wait_ge(sem, val)
            return body

        with nc.Block() as block:
            block.sync(make("sp"))
            block.scalar(make("act"))
            block.vector(make("dve"))
            block.gpsimd(make("pool"))
            block.tensor(make("pe"))


def blk(name, c, lo, hi):
    return [(name, c, b) for b in range(lo // 512, (hi - 1) // 512 + 1)]


def blks(name, cs, lo, hi):
    r = []
    for c in cs:
        r += blk(name, c, lo, hi)
    return r


def build_program():
    nc = bass.Bass("TRN2", target_bir_lowering=False)
    S = Sched()

    def din(name, shape, dt=F32):
        return nc.dram_tensor(name, list(shape), dt, kind="ExternalInput").ap()

    xa = {"p": din("xa_p", [S_P, D]), "s": din("xa_s", [S_S, D])}
    xe = {"p": din("xe_p", [18 * 128, D]), "s": din("xe_s", [18 * 128, D])}
    maskd = {"p": din("mask_p", [128, 8, 6]), "s": din("mask_s", [128, 8, 6])}
    w_in_ab = din("w_in_ab", [D, 1536])
    w_out_ab = din("w_out_ab", [D, D])
    w_in_c = din("w_in_c", [D, 3072])
    w_out_c = din("w_out_c", [D, D])
    w_up = din("w_up", [2, D, 2 * DFF])
    w_down = din("w_down", [2, DFF, D])
    g0bc_d = din("g0bc", [128, D])
    gainbc_d = din("gainbc", [128, 512])
    bst_d = din("bst", [128, 4])
    wsT_d = din("wsT", [128, 4, 128])
    gvec_d = din("gvec", [128, 4, 8])
    ccw_d = din("ccw", [128, 3, 8])
    ccb_d = din("ccb", [128, 8])
    fcw_d = din("fcw", [128, 2, 3, 44])
    fcb_d = din("fcb", [128, 2, 44])
    identf_d = din("identf", [128, 128])
    identb_d = din("identb", [128, 128], BF16)
    WA_d = {"p": din("WA_p", [16, 32], BF16), "s": din("WA_s", [128, 256], BF16)}
    MB1_d = {"p": din("MB1_p", [128, 16, 256], BF16), "s": din("MB1_s", [128, 128, 36], BF16)}
    MB2_d = {"p": din("MB2_p", [128, 16, 256], BF16), "s": din("MB2_s", [128, 128, 36], BF16)}
    CD_d = din("CD", [128, 2, 128], BF16)
    MT_d = din("MT_p", [4, 2, 128, 16 * 512], BF16)
    yout = {"p": nc.dram_tensor("y_p", [S_P, D], F32, kind="ExternalOutput").ap(),
            "s": nc.dram_tensor("y_s", [S_P, D], F32, kind="ExternalOutput").ap()}
    Fd = {"p": nc.dram_tensor("F_p", [4, S_P, 128], BF16).ap(),
          "s": nc.dram_tensor("F_s", [4, S_S, 128], BF16).ap()}

    XR = nc.alloc_sbuf_tensor("XR", [128, 8 * W], F32)
    HR = nc.alloc_sbuf_tensor("HR", [128, 8 * 2056], BF16)
    BR = nc.alloc_sbuf_tensor("BR", [128, 8 * W], BF16)
    WST = [nc.alloc_sbuf_tensor("WST%d" % i, [128, 3072], F32) for i in range(2)]
    WBF = [nc.alloc_sbuf_tensor("WBF%d" % i, [128, 3072], BF16) for i in range(2)]
    SCR = nc.alloc_sbuf_tensor("SCR", [128, 17408], BF16)
    PS = nc.alloc_psum_tensor("PS", [128, 8 * 512], F32)
    bst = nc.alloc_sbuf_tensor("bst_s", [128, 4], F32)
    wsT = nc.alloc_sbuf_tensor("wsT_s", [128, 512], BF16)
    gvec = nc.alloc_sbuf_tensor("gvec_s", [128, 32], F32)
    ccw = nc.alloc_sbuf_tensor("ccw_s", [128, 24], F32)
    ccb = nc.alloc_sbuf_tensor("ccb_s", [128, 8], F32)
    fcw = nc.alloc_sbuf_tensor("fcw_s", [128, 264], F32)
    fcb = nc.alloc_sbuf_tensor("fcb_s", [128, 88], F32)
    identf = nc.alloc_sbuf_tensor("identf_s", [128, 128], F32)
    identb = nc.alloc_sbuf_tensor("identb_s", [128, 128], BF16)
    onesb = nc.alloc_sbuf_tensor("onesb", [128, 128], BF16)
    epsT = nc.alloc_sbuf_tensor("epsT", [128, 1], F32)
    maskt = nc.alloc_sbuf_tensor("maskt", [128, 48], F32)
    CDt = nc.alloc_sbuf_tensor("CDt", [128, 256], BF16)
    WAt = nc.alloc_sbuf_tensor("WAt", [128, 256], BF16)
    stat = nc.alloc_sbuf_tensor("stat", [128, 64], F32)

    X = XR[:, :].rearrange("p (c n) -> p c n", c=8)
    H = HR[:, :].rearrange("p (c n) -> p c n", c=8)
    BIG = BR[:, :].rearrange("p (c n) -> p c n", c=8)

    def bank(b, n=512):
        return PS[:, b * 512:b * 512 + n]

    bstate = {"b": 0}

    def nb():
        b = bstate["b"]
        bstate["b"] = (b + 1) % 8
        return b

    def nb2():
        if bstate["b"] % 2:
            bstate["b"] = (bstate["b"] + 1) % 8
        b = bstate["b"]
        bstate["b"] = (b + 2) % 8
        return b

    cnt = {"k": 0}

    def uid():
        cnt["k"] += 1
        return cnt["k"]

    def ld(dst, src, name):
        S.dma("sp", lambda e, d=dst, s=src: e.dma_start(out=d, in_=s), [], [name], ("c", name))

    ld(bst[:, :], bst_d[:, :], "bst")
    ld(WST[0][:, 0:512], wsT_d.rearrange("q h p -> q (h p)"), "wsTf")
    ld(gvec[:, :], gvec_d.rearrange("p a b -> p (a b)"), "gvec")
    ld(ccw[:, :], ccw_d.rearrange("p a b -> p (a b)"), "ccw")
    ld(ccb[:, :], ccb_d[:, :], "ccb")
    ld(fcw[:, :], fcw_d.rearrange("p l a b -> p (l a b)"), "fcw")
    ld(fcb[:, :], fcb_d.rearrange("p l b -> p (l b)"), "fcb")
    ld(identf[:, :], identf_d[:, :], "identf")
    ld(identb[:, :], identb_d[:, :], "identb")
    ld(CDt[:, :], CD_d.rearrange("p a b -> p (a b)"), "CD")
    S.op("pool", lambda e: e.memset(onesb[:, :], 1.0), [], ["ones"])
    S.op("pool", lambda e: e.memset(epsT[:, :], EPS), [], ["eps"])
    S.op("pool", lambda e: e.memset(BR[:, :], 0.0), [], ["BRz"])
    S.op("pool", lambda e: e.memset(HR[:, :], 0.0), [], ["HRz"])
    S.op("dve", lambda e: e.tensor_copy(out=wsT[:, :], in_=WST[0][:, 0:512]), ["wsTf"], ["wsT"])
    CONSTS = ["g0bc", "gainbc", "bst", "wsT", "gvec", "ccw", "ccb", "fcw", "fcb", "identf", "identb", "CD",
              "ones", "eps", "BRz", "HRz"]

    wplan = []
    wstate = {"i": 0, "issued": 0}

    def _issue(k):
        pieces, ncols = wplan[k]
        s = k % 2
        off = 0
        for pi, src in enumerate(pieces):
            a, b = src.shape[1], src.shape[2]
            dst = WST[s][:, off:off + a * b].rearrange("p (a b) -> p a b", a=a)
            S.dma("sp", lambda e, d=dst, sr=src: e.dma_start(out=d, in_=sr), [], [("wst", s, pi)], ("wst", s, pi))
            off += a * b
        assert off == ncols and ncols <= 3072
        S.op("act", lambda e, s=s, n=ncols: e.copy(out=WBF[s][:, 0:n], in_=WST[s][:, 0:n]),
             [("wst", s, pi) for pi in range(3)], [("wbf", s)] + [("wst", s, pi) for pi in range(3)])

    def wtile(pieces, ncols):
        k = wstate["i"]
        wstate["i"] += 1
        assert wplan[k][1] == ncols and len(wplan[k][0]) == len(pieces), (k, wplan[k][1], ncols)
        while wstate["issued"] <= min(k + 1, len(wplan) - 1):
            _issue(wstate["issued"])
            wstate["issued"] += 1
        return WBF[k % 2], ("wbf", k % 2)

    def win_srcs(lo, n):
        return [([w_in_ab[:, lo + p0:lo + p0 + min(384, n - p0)].rearrange("(c p) o -> p c o", p=128)],
                 8 * min(384, n - p0)) for p0 in range(0, n, 384)]

    def wout_ab_src(o):
        return w_out_ab[:, o * 128:(o + 1) * 128].rearrange("(k p) o -> p k o", p=128)

    def wout_c_src(o):
        return w_out_c[:, o * 128:(o + 1) * 128].rearrange("(k p) o -> p k o", p=128)

    def ffn_up_srcs(l, i):
        return [w_up[l][:, i * 128:(i + 1) * 128].rearrange("(c p) o -> p c o", p=128),
                w_up[l][:, DFF + i * 128:DFF + (i + 1) * 128].rearrange("(c p) o -> p c o", p=128)]

    def ffn_down_src(l, k0, nk, o):
        return w_down[l][k0 * 128:(k0 + nk) * 128, o * 128:(o + 1) * 128].rearrange("(k p) o -> p k o", p=128)

    def mixc_srcs(j):
        return [w_in_c[:, kk * 1024 + j * 128:kk * 1024 + (j + 1) * 128].rearrange("(c p) o -> p c o", p=128)
                for kk in range(3)]

    def plan_ffn(l):
        up = lambda i: (ffn_up_srcs(l, i), 2048)
        dn = lambda grp: [([ffn_down_src(l, grp[0], len(grp), o)], len(grp) * 128) for o in range(8)]
        r = [up(i) for i in GROUPS[0]]
        for g in range(1, len(GROUPS)):
            r += [up(i) for i in GROUPS[g][:2]]
            r += dn(GROUPS[g - 1])
            r += [up(i) for i in GROUPS[g][2:]]
        r += dn(GROUPS[-1])
        r += dn(GROUPS[-1])
        return r

    for _slab in ("s", "p"):
        wplan += win_srcs(1024, 512)
        wplan += win_srcs(0, 1024)
        wplan += [([wout_ab_src(o)], 1024) for o in range(8)] * 2
        wplan += plan_ffn(0)
        wplan += [(mixc_srcs(j), 3072) for j in range(8)]
        wplan += [([wout_c_src(o)], 1024) for o in range(8)] * 2
        wplan += plan_ffn(1)

    def scr_f32(off, n):
        return SCR[:, off:off + 2 * n].bitcast(F32)

    FP_XT = [scr_f32(s_ * 4608, 1024) for s_ in range(3)]
    FP_XN = [SCR[:, s_ * 4608 + 2048:s_ * 4608 + 3072] for s_ in range(3)]
    FP_HT = [SCR[:, s_ * 4608 + 3072:s_ * 4608 + 4096] for s_ in range(3)]
    FB = [SCR[:, s_ * 4608 + 4096:s_ * 4608 + 4608] for s_ in range(3)]
    FR_XT = [scr_f32(s_ * 2048, 1024) for s_ in range(3)]
    FR_XN = [SCR[:, 6144 + s_ * 1024:6144 + (s_ + 1) * 1024] for s_ in range(3)]
    FR_HT = [SCR[:, 9216 + s_ * 1024:9216 + (s_ + 1) * 1024] for s_ in range(2)]
    VV2 = [scr_f32(11264 + s_ * 1024, 512) for s_ in range(2)]
    VN3 = [SCR[:, 13312:13824], SCR[:, 13824:14336], HR[:, 15360:15872]]
    UU3 = [HR[:, 12288 + s_ * 1024:12288 + (s_ + 1) * 1024].bitcast(F32) for s_ in range(3)]
    AA1 = HR[:, 15872:16384]
    g0bc = scr_f32(14336, 1024)
    gainbc = scr_f32(16384, 512)

    def load_gconsts():
        ld(g0bc, g0bc_d[:, :], "g0bc")
        ld(gainbc, gainbc_d[:, :], "gainbc")
    WIN = HR[:, 0:8 * 1536].rearrange("p (c n) -> p c n", c=8)

    def load_win(cols_lo, cols_n):
        toks = []
        for (pieces, ncols), p0 in zip(win_srcs(cols_lo, cols_n), range(0, cols_n, 384)):
            pn = ncols // 8
            wb, tok = wtile(pieces, ncols)
            S.op("act", lambda e, wb=wb, p0=p0, pn=pn: e.copy(
                out=WIN[:, :, p0:p0 + pn], in_=wb[:, 0:8 * pn].rearrange("p (c n) -> p c n", c=8)),
                [tok], [("win", p0)])
            toks.append(("win", p0))
        return toks

    def newton_rsqrt(a, b, c, n, tin, ty, tt_):
        xs, ys, ts = stat[:, a:a + n], stat[:, b:b + n], stat[:, c:c + n]
        xi, yi = xs.bitcast(I32), ys.bitcast(I32)
        S.op("dve", lambda e: e.tensor_scalar(out=xs, in0=xs, scalar1=EPS, scalar2=None, op0=ALU.add), [tin], [tin])
        S.op("dve", lambda e: e.tensor_scalar(out=yi, in0=xi, scalar1=1, scalar2=None, op0=ALU.arith_shift_right),
             [tin], [ty])
        S.op("dve", lambda e: e.tensor_scalar(out=yi, in0=yi, scalar1=-1, scalar2=0x5f3759df, op0=ALU.mult,
                                              op1=ALU.add), [ty], [ty])
        for _ in range(2):
            S.op("dve", lambda e: e.tensor_tensor(out=ts, in0=ys, in1=ys, op=ALU.mult), [ty], [tt_])
            S.op("dve", lambda e: e.scalar_tensor_tensor(out=ts, in0=ts, scalar=-0.5, in1=xs, op0=ALU.mult,
                                                         op1=ALU.mult), [tt_, tin], [tt_])
            S.op("dve", lambda e: e.scalar_tensor_tensor(out=ys, in0=ts, scalar=1.5, in1=ys, op0=ALU.add,
                                                         op1=ALU.mult), [ty, tt_], [ty])

    def chunk_A(src_rows, q, XT, XN, newton=False):
        sl = q % len(XT)
        S.dma("sp", lambda e, sl=sl, sr=src_rows: e.dma_start(out=XT[sl], in_=sr), [], [("xt", sl)], ("xt", sl))
        S.op("act", lambda e, sl=sl: e.activation(out=XN[sl], in_=XT[sl], func=AF.Square, scale=float(D ** -0.5),
                                                   accum_out=stat[:, 3 * sl:3 * sl + 1]),
             [("xt", sl)], [("xn", sl), ("ms", sl)])
        if newton:
            newton_rsqrt(3 * sl, 3 * sl + 2, 3 * sl + 1, 1, ("ms", sl), ("rs", sl), ("sd", sl))
        else:
            S.op("act", lambda e, sl=sl: e.activation(out=stat[:, 3 * sl + 1:3 * sl + 2],
                                                       in_=stat[:, 3 * sl:3 * sl + 1],
                                                       func=AF.Sqrt, bias=epsT[:, 0:1], scale=1.0),
                 [("ms", sl), "eps"], [("sd", sl)])
            S.op("dve", lambda e, sl=sl: e.reciprocal(out=stat[:, 3 * sl + 2:3 * sl + 3],
                                                      in_=stat[:, 3 * sl + 1:3 * sl + 2]),
                 [("sd", sl)], [("rs", sl)])
        S.op("dve", lambda e, sl=sl: e.scalar_tensor_tensor(out=XN[sl], in0=XT[sl],
                                                            scalar=stat[:, 3 * sl + 2:3 * sl + 3],
                                                            in1=g0bc, op0=ALU.mult, op1=ALU.mult),
             [("xt", sl), ("rs", sl), "g0bc"], [("xn", sl)])

    def chunk_B(q, XN, HT, b=None):
        sl = q % len(XN)
        if b is None:
            b = nb()
        pb = bank(b).bitcast(BF16)

        def tr(e, sl=sl, pb=pb):
            ins = None
            for c in range(8):
                ins = e.transpose(out=pb[:, c * 128:(c + 1) * 128], in_=XN[sl][:, c * 128:(c + 1) * 128],
                                  identity=identb[:, :])
            return ins
        S.op("pe", tr, [("xn", sl), "identb"], [("ps", b)])
        S.op("act", lambda e, sl=sl, pb=pb: e.copy(out=HT[sl], in_=pb[:, 0:1024]), [("ps", b)], [("ht", sl)])

    FPR = XR[:, 0:4096].bitcast(BF16).rearrange("p (s c) -> p s c", s=16)

    def f_pass(slab, nchunks):
        load_gconsts()
        wt = load_win(1024, 512)

        def stage_C(q):
            sl = q % 3
            b = nb()

            def mm(e, sl=sl, b=b):
                ins = None
                for c in range(8):
                    ins = e.matmul(bank(b), lhsT=FP_HT[sl][:, c * 128:(c + 1) * 128], rhs=WIN[:, c, 0:512],
                                   start=(c == 0), stop=(c == 7))
                return ins
            S.op("pe", mm, [("ht", sl)] + wt, [("ps", b)])
            if slab == "p":
                S.op("dve", lambda e, q=q, b=b: e.tensor_copy(out=FPR[:, q, :], in_=bank(b)),
                     [("ps", b)], [("F", slab, q)])
                return
            S.op("dve", lambda e, sl=sl, b=b: e.tensor_copy(out=FB[sl], in_=bank(b)), [("ps", b)], [("fb", sl)])
            S.dma("pool", lambda e, sl=sl, q=q: e.dma_start(
                out=Fd[slab][:, q * 128:(q + 1) * 128, :].rearrange("g t c -> t g c"),
                in_=FB[sl].rearrange("p (g c) -> p g c", g=4)),
                [("fb", sl)], [("F", slab, q)], ("fst", sl))

        for t in range(nchunks + 2):
            if t < nchunks:
                chunk_A(xa[slab][t * 128:(t + 1) * 128, :], t, FP_XT, FP_XN)
            if 0 <= t - 1 < nchunks:
                chunk_B(t - 1, FP_XN, FP_HT)
            if 0 <= t - 2 < nchunks:
                stage_C(t - 2)

    def front(slab):
        load_gconsts()
        wt = load_win(0, 1024)
        NQ = 18

        def geom(q):
            lo = max(0, q * 128 - E0)
            hi = min(W, (q + 1) * 128 - E0)
            return lo, hi, lo + E0 - q * 128, hi - lo

        def sb(p):
            return 16 + p * 16

        def st_A1_dma(q):
            sl = q % 3
            S.dma("sp", lambda e, sl=sl, q=q: e.dma_start(out=FR_XT[sl], in_=xe[slab][q * 128:(q + 1) * 128, :]),
                  [], [("xt", sl)], ("xt", sl))

        def st_A1(q):
            sl = q % 3
            S.op("act", lambda e, sl=sl: e.activation(out=FR_XN[sl], in_=FR_XT[sl], func=AF.Square,
                                                       scale=float(D ** -0.5),
                                                       accum_out=stat[:, sb(q % 2) + 4:sb(q % 2) + 5]),
                 [("xt", sl)], [("xn", sl), ("msb", q % 2)])

        def st_A2(q):
            sl = q % 3
            yc = sb(q % 2) + 9
            S.op("dve", lambda e, sl=sl, yc=yc: e.scalar_tensor_tensor(out=FR_XN[sl], in0=FR_XT[sl],
                                                                       scalar=stat[:, yc:yc + 1],
                                                                       in1=g0bc, op0=ALU.mult, op1=ALU.mult),
                 [("xt", sl), ("yb", q % 2), "g0bc"], [("xn", sl)])

        def st_B(q):
            sl = q % 3
            hs = q % 2
            lo, hi, i0, n = geom(q)
            b = 0 if q % 2 == 0 else 7
            pb = bank(b).bitcast(BF16)

            def tr(e, sl=sl, pb=pb):
                ins = None
                for c in range(8):
                    ins = e.transpose(out=pb[:, c * 128:(c + 1) * 128], in_=FR_XN[sl][:, c * 128:(c + 1) * 128],
                                      identity=identb[:, :])
                return ins
            S.op("pe", tr, [("xn", sl), "identb"], [("ps", b)])
            S.op("act", lambda e, hs=hs, pb=pb: e.copy(out=FR_HT[hs], in_=pb[:, 0:1024]), [("ps", b)], [("ht", hs)])
            b2 = 1

            def trx(e, sl=sl, b2=b2):
                ins = None
                for c in range(8):
                    ins = e.transpose(out=PS[:, b2 * 512 + c * 128:b2 * 512 + (c + 1) * 128],
                                      in_=FR_XT[sl][:, c * 128:(c + 1) * 128], identity=identf[:, :])
                return ins
            S.op("pe", trx, [("xt", sl), "identf"], [("ps", b2), ("ps", b2 + 1)])
            S.op("act", lambda e, b2=b2, lo=lo, n=n, i0=i0: e.copy(
                out=X[:, :, lo:lo + n],
                in_=PS[:, b2 * 512:b2 * 512 + 1024].rearrange("p (c t) -> p c t", c=8)[:, :, i0:i0 + n]),
                [("ps", b2), ("ps", b2 + 1)], blks("X", range(8), lo, hi))

        def st_C1(q):
            hs, u3, v2 = q % 2, q % 3, q % 2
            UU, VV, VN = UU3[u3], VV2[v2], VN3[u3]
            so = sb((q + 3) % 2)
            bu, bv = 3, 4

            def mmu(e, hs=hs, bu=bu):
                ins = None
                for c in range(8):
                    ins = e.matmul(bank(bu), lhsT=FR_HT[hs][:, c * 128:(c + 1) * 128], rhs=WIN[:, c, 0:512],
                                   start=(c == 0), stop=(c == 7))
                return ins

            def mmv(e, hs=hs, bv=bv):
                ins = None
                for c in range(8):
                    ins = e.matmul(bank(bv), lhsT=FR_HT[hs][:, c * 128:(c + 1) * 128], rhs=WIN[:, c, 512:1024],
                                   start=(c == 0), stop=(c == 7))
                return ins
            S.op("pe", mmu, [("ht", hs)] + wt, [("ps", bu)])
            S.op("pe", mmv, [("ht", hs)] + wt, [("ps", bv)])
            S.op("act", lambda e, bu=bu, UU=UU: e.activation(out=UU, in_=bank(bu), func=AF.Gelu_apprx_tanh),
                 [("ps", bu)], [("uu", u3)])
            S.op("act", lambda e, bv=bv, VV=VV: e.activation(out=VV, in_=bank(bv), func=AF.Gelu_apprx_tanh),
                 [("ps", bv)], [("vv", v2)])
            for h in range(4):
                S.op("act", lambda e, h=h, VV=VV, VN=VN, so=so: e.activation(
                    out=VN[:, h * 128:(h + 1) * 128], in_=VV[:, h * 128:(h + 1) * 128],
                    func=AF.Square, scale=float(128 ** -0.5), accum_out=stat[:, so + h:so + h + 1]),
                    [("vv", v2)], [("vn", u3, h), ("msb", (q + 3) % 2)])

        def st_C2(q):
            u3, v2 = q % 3, q % 2
            VV, VN = VV2[v2], VN3[u3]
            pb_ = (q + 3) % 2
            so = sb(pb_)
            for h in range(4):
                S.op("dve", lambda e, h=h, VV=VV, VN=VN, so=so: e.scalar_tensor_tensor(
                    out=VN[:, h * 128:(h + 1) * 128], in0=VV[:, h * 128:(h + 1) * 128],
                    scalar=stat[:, so + 5 + h:so + 6 + h], in1=gainbc[:, h * 128:(h + 1) * 128],
                    op0=ALU.mult, op1=ALU.mult), [("vv", v2), ("yb", pb_), "gainbc"], [("vn", u3, h)])

        def st_D(q):
            u3 = q % 3
            UU, VN, AA = UU3[u3], VN3[u3], AA1
            lo, hi, i0, n = geom(q)
            bs_ = 5

            def mms(e, bs_=bs_, VN=VN):
                ins = None
                for h in range(4):
                    ins = e.matmul(PS[:, bs_ * 512 + h * 128:bs_ * 512 + (h + 1) * 128],
                                   lhsT=wsT[:, h * 128:(h + 1) * 128], rhs=VN[:, h * 128:(h + 1) * 128],
                                   start=True, stop=True)
                return ins
            S.op("pe", mms, [("vn", u3, h) for h in range(4)] + ["wsT"], [("ps", bs_)])
            for h in range(4):
                S.op("dve", lambda e, h=h, bs_=bs_, AA=AA, UU=UU: e.scalar_tensor_tensor(
                    out=AA[:, h * 128:(h + 1) * 128], in0=PS[:, bs_ * 512 + h * 128:bs_ * 512 + (h + 1) * 128],
                    scalar=bst[:, h:h + 1], in1=UU[:, h * 128:(h + 1) * 128], op0=ALU.add, op1=ALU.mult),
                    [("ps", bs_), ("uu", u3), "bst"], [("aa", h)])
        def st_Db(q):
            AA = AA1
            lo, hi, i0, n = geom(q)
            ba = 6
            pba = bank(ba).bitcast(BF16)

            def tra(e, pba=pba, AA=AA):
                ins = None
                for h in range(4):
                    ins = e.transpose(out=pba[:, h * 128:(h + 1) * 128], in_=AA[:, h * 128:(h + 1) * 128],
                                      identity=identb[:, :])
                return ins
            S.op("pe", tra, [("aa", h) for h in range(4)] + ["identb"], [("ps", ba)])
            S.op("act", lambda e, pba=pba, lo=lo, n=n, i0=i0: e.copy(
                out=BIG[:, 0:4, lo:lo + n],
                in_=pba[:, 0:512].rearrange("p (c t) -> p c t", c=4)[:, :, i0:i0 + n]),
                [("ps", ba)], blks("BIG", range(4), lo, hi))

        S.op("dve", lambda e: e.memset(stat[:, 16:48], 1.0), [], [("msb", 0), ("msb", 1), ("yb", 0), ("yb", 1)])
        for t in range(NQ + 5):
            if t < NQ:
                st_A1_dma(t)
            if 0 <= t - 5 < NQ:
                st_D(t - 5)
            if 0 <= t - 4 < NQ:
                st_C2(t - 4)
            if 0 <= t - 1 < NQ:
                st_A2(t - 1)
            if 0 <= t - 2 < NQ:
                st_B(t - 2)
            if t < NQ:
                st_A1(t)
            if 0 <= t - 3 < NQ:
                st_C1(t - 3)
            if 0 <= t - 5 < NQ:
                st_Db(t - 5)
            p = t % 2
            newton_rsqrt(sb(p), sb(p) + 5, sb(p) + 10, 5, ("msb", p), ("yb", p), ("tb", p))

    def dft_dense_p():
        MT = [SCR[:, 0:8192].rearrange("p (s k) -> p s k", s=16), SCR[:, 8192:16384].rearrange("p (s k) -> p s k", s=16)]
        PB = HR[:, 0:4096].rearrange("p (g r k) -> p g r k", g=4, r=2)
        ftoks = [("F", "p", q) for q in range(16)]
        it = 0
        for kt in range(4):
            for ri in range(2):
                ms = it % 2
                it += 1
                S.dma("sp", lambda e, ms=ms, kt=kt, ri=ri: e.dma_start(
                    out=SCR[:, ms * 8192:(ms + 1) * 8192], in_=MT_d[kt, ri]), [], [("mt", ms)], ("mt", ms))
                for g in range(4):
                    b = nb()

                    def mm(e, ms=ms, g=g, b=b):
                        ins = None
                        for s2 in range(16):
                            ins = e.matmul(bank(b), lhsT=FPR[:, s2, g * 128:(g + 1) * 128], rhs=MT[ms][:, s2, :],
                                           start=(s2 == 0), stop=(s2 == 15))
                        return ins
                    S.op("pe", mm, ftoks + [("mt", ms)], [("ps", b)])
                    if (g + ri) % 2 == 0:
                        S.op("act", lambda e, g=g, ri=ri, b=b: e.copy(out=PB[:, g, ri, :], in_=bank(b)),
                             [("ps", b)], [("pb", g, ri)])
                    else:
                        S.op("dve", lambda e, g=g, ri=ri, b=b: e.tensor_copy(out=PB[:, g, ri, :], in_=bank(b)),
                             [("ps", b)], [("pb", g, ri)])
            for g in range(4):
                b = nb()

                def mmc(e, g=g, b=b):
                    e.matmul(bank(b), lhsT=CDt[:, 0:128], rhs=PB[:, g, 0, :], start=True, stop=False)
                    return e.matmul(bank(b), lhsT=CDt[:, 128:256], rhs=PB[:, g, 1, :], start=False, stop=True)
                S.op("pe", mmc, [("pb", g, 0), ("pb", g, 1), "CD"], [("ps", b)])
                e0 = HALO + kt * 512
                S.op("act", lambda e, g=g, b=b, e0=e0: e.copy(out=BIG[:, 4 + g, e0:e0 + 512], in_=bank(b)),
                     [("ps", b)], blk("BIG", 4 + g, e0, e0 + 512))

    def dft(slab):
        if slab == "p":
            return dft_dense_p()
        n = 16 if slab == "p" else 128
        nk1 = 128 if slab == "p" else 18
        NB = 2 * nk1
        NK = nk1 * n
        S_len = n * 128
        MB1 = HR[:, 0:n * NB].rearrange("p (k j) -> p k j", k=n)
        MB2 = HR[:, 4608:4608 + n * NB].rearrange("p (k j) -> p k j", k=n)
        P = HR[:, 9216:9216 + 2 * NK].rearrange("p (r k) -> p r k", r=2)
        S.dma("sp", lambda e: e.dma_start(out=HR[:, 0:n * NB], in_=MB1_d[slab].rearrange("p k j -> p (k j)")),
              [], ["MB1"], ("c", "MB1"))
        S.dma("sp", lambda e: e.dma_start(out=HR[:, 4608:4608 + n * NB],
                                          in_=MB2_d[slab].rearrange("p k j -> p (k j)")),
              [], ["MB2"], ("c", "MB2"))
        S.dma("sp", lambda e: e.dma_start(out=WAt[0:n, 0:2 * n], in_=WA_d[slab][:, :]), [], ["WA"], ("c", "WA"))
        Y = XR[:, 0:16384].bitcast(BF16).rearrange("p (c k) -> p c k", c=128)
        XB = SCR[:, 0:16384].rearrange("p (s c) -> p s c", s=128)
        cpb = 512 // (2 * n)
        kpb = 512 // NB
        if slab == "p":
            e_lo, idx_lo, ncols = 3, 0, 2048
        else:
            e_lo, idx_lo, ncols = 0, 125, 2054
        for g in range(4):
            S.dma("sp", lambda e, g=g: e.dma_start(
                out=XB[0:n, :, :], in_=Fd[slab][g].rearrange("(s2 s1) c -> s2 s1 c", s1=128)),
                [("F", slab, q) for q in range(n)], [("xb",)], ("xb",))
            for c0 in range(0, 128, cpb):
                b = nb()

                def mma(e, c0=c0, b=b):
                    ins = None
                    for j in range(cpb):
                        ins = e.matmul(PS[:, b * 512 + j * 2 * n:b * 512 + (j + 1) * 2 * n],
                                       lhsT=XB[0:n, :, c0 + j], rhs=WAt[0:n, 0:2 * n], start=True, stop=True)
                    return ins
                S.op("pe", mma, [("xb",), "WA"], [("ps", b)])
                eng = "act" if (c0 // cpb) % 2 == 0 else "dve"
                if eng == "act":
                    S.op("act", lambda e, c0=c0, b=b: e.copy(
                        out=Y[:, c0:c0 + cpb, 0:2 * n],
                        in_=bank(b).rearrange("p (c k) -> p c k", c=cpb)), [("ps", b)], [("Y", c0)])
                else:
                    S.op("dve", lambda e, c0=c0, b=b: e.tensor_copy(
                        out=Y[:, c0:c0 + cpb, 0:2 * n],
                        in_=bank(b).rearrange("p (c k) -> p c k", c=cpb)), [("ps", b)], [("Y", c0)])
            ytoks = [("Y", c0) for c0 in range(0, 128, cpb)]
            for k0 in range(0, n, kpb):
                kn = min(kpb, n - k0)
                b = nb()

                def mmb(e, k0=k0, kn=kn, b=b):
                    ins = None
                    for j in range(kn):
                        k2 = k0 + j
                        o = PS[:, b * 512 + j * NB:b * 512 + (j + 1) * NB]
                        e.matmul(o, lhsT=Y[:, :, k2], rhs=MB1[:, k2, :], start=True, stop=False)
                        ins = e.matmul(o, lhsT=Y[:, :, n + k2], rhs=MB2[:, k2, :], start=False, stop=True)
                    return ins
                S.op("pe", mmb, ytoks + ["MB1", "MB2"], [("ps", b)])
                src = PS[:, b * 512:b * 512 + kn * NB].rearrange("p (k r i) -> p k r i", k=kn, r=2)
                dst = P.rearrange("p r (i k) -> p k r i", k=n)[:, k0:k0 + kn, :, :]
                if (k0 // kpb) % 2 == 0:
                    S.op("act", lambda e, s=src, d=dst: e.copy(out=d, in_=s), [("ps", b)], [("P", k0)])
                else:
                    S.op("dve", lambda e, s=src, d=dst: e.tensor_copy(out=d, in_=s), [("ps", b)], [("P", k0)])
            ptoks = [("P", k0) for k0 in range(0, n, kpb)]
            t0 = 0
            while t0 < ncols:
                tn = min(512, ncols - t0)
                b = nb()

                def mmc(e, t0=t0, tn=tn, b=b):
                    e.matmul(bank(b, tn), lhsT=CDt[:, 0:128], rhs=P[:, 0, idx_lo + t0:idx_lo + t0 + tn],
                             start=True, stop=False)
                    return e.matmul(bank(b, tn), lhsT=CDt[:, 128:256], rhs=P[:, 1, idx_lo + t0:idx_lo + t0 + tn],
                                    start=False, stop=True)
                S.op("pe", mmc, ptoks + ["CD"], [("ps", b)])
                S.op("act", lambda e, t0=t0, tn=tn, b=b, g=g: e.copy(
                    out=BIG[:, 4 + g, e_lo + t0:e_lo + t0 + tn], in_=bank(b, tn)),
                    [("ps", b)], blk("BIG", 4 + g, e_lo + t0, e_lo + t0 + tn))
                t0 += tn

    def mask_left():
        S.op("dve", lambda e: e.tensor_tensor(out=X[:, :, 0:3], in0=X[:, :, 0:3],
                                              in1=maskt[:, 0:48].rearrange("p (c m) -> p c m", c=8)[:, :, 0:3],
                                              op=ALU.mult),
             blks("X", range(8), 0, 3) + ["mask"], blks("X", range(8), 0, 3))

    def mask_right():
        S.op("dve", lambda e: e.tensor_tensor(out=X[:, :, W - 3:W], in0=X[:, :, W - 3:W],
                                              in1=maskt[:, 0:48].rearrange("p (c m) -> p c m", c=8)[:, :, 3:6],
                                              op=ALU.mult),
             blks("X", range(8), W - 3, W) + ["mask"], blks("X", range(8), W - 3, W))

    def mask_halo():
        mask_left()
        mask_right()

    SQ = SCR[:, 0:4096].rearrange("p (c n) -> p c n", c=8)
    SD = scr_f32(4096, 512)
    CG = [scr_f32(6144 + i_ * 1056, 528) for i_ in (0, 1)]
    CV = [scr_f32(6144 + i_ * 1056, 528) for i_ in (2, 3)]
    SG = [scr_f32(6144 + i_ * 1056, 528) for i_ in (4, 5)]
    YT = scr_f32(12288, 1024)
    OST = [scr_f32(14336, 1024)]

    RS2 = [scr_f32(5120, 512), scr_f32(16384, 512)]
    rstate = {"i": 0}

    def rms_tile(t0, tn):
        par = rstate["i"] % 2
        rstate["i"] += 1
        RSp = RS2[par]
        S.op("act", lambda e: e.activation(out=SQ[:, :, 0:tn], in_=X[:, :, t0:t0 + tn], func=AF.Square),
             blks("X", range(8), t0, t0 + tn), [("sq",)])
        b = nb()

        def mm(e, b=b):
            ins = None
            for c in range(8):
                ins = e.matmul(bank(b, tn), lhsT=onesb[:, :], rhs=SQ[:, c, 0:tn], start=(c == 0), stop=(c == 7))
            return ins
        S.op("pe", mm, [("sq",), "ones"], [("ps", b)])
        S.op("act", lambda e, b=b: e.activation(out=SD[:, 0:tn], in_=bank(b, tn), func=AF.Ln,
                                                 bias=epsT[:, 0:1], scale=float(1.0 / D)),
             [("ps", b), "eps"], [("sd",)])
        S.op("act", lambda e: e.activation(out=RSp[:, 0:tn], in_=SD[:, 0:tn], func=AF.Exp, scale=-0.5),
             [("sd",)], [("rs", par)])
        return RSp, ("rs", par)

    def norm_to_H(gi):
        S.op("pool", lambda e: e.memset(H[:, :, 0:1], 0.0), [], blks("H", range(8), 0, 1))
        S.op("pool", lambda e: e.memset(H[:, :, 2055:2056], 0.0), [], blks("H", range(8), 2055, 2056))
        for (t0, tn) in TT:
            RSp, rtok = rms_tile(t0, tn)
            for c in range(8):
                S.op("dve", lambda e, c=c, t0=t0, tn=tn, RSp=RSp: e.scalar_tensor_tensor(
                    out=H[:, c, 1 + t0:1 + t0 + tn], in0=X[:, c, t0:t0 + tn],
                    scalar=gvec[:, gi * 8 + c:gi * 8 + c + 1], in1=RSp[:, 0:tn], op0=ALU.mult, op1=ALU.mult),
                    blk("X", c, t0, t0 + tn) + [rtok, "gvec"], blk("H", c, 1 + t0, 1 + t0 + tn))

    def linear_to_X(wsrc_fn, nk, rhs_fn, rhs_toks_fn, tiles=None):
        for o in range(8):
            wb, tok = wtile([wsrc_fn(o)], nk * 128)
            for (t0, tn) in (tiles or TT):
                b = nb()

                def mm(e, wb=wb, t0=t0, tn=tn, b=b):
                    ins = None
                    for k in range(nk):
                        ins = e.matmul(bank(b, tn), lhsT=wb[:, k * 128:(k + 1) * 128], rhs=rhs_fn(k, t0, tn),
                                       start=(k == 0), stop=(k == nk - 1))
                    return ins
                S.op("pe", mm, [tok] + rhs_toks_fn(t0, tn), [("ps", b)])
                S.op("dve", lambda e, o=o, t0=t0, tn=tn, b=b: e.tensor_tensor(
                    out=X[:, o, t0:t0 + tn], in0=bank(b, tn), in1=X[:, o, t0:t0 + tn], op=ALU.add),
                    [("ps", b)] + blk("X", o, t0, t0 + tn), blk("X", o, t0, t0 + tn))

    def ffn(l):
        norm_to_H(1 + l)
        fw = fcw[:, l * 132:(l + 1) * 132].rearrange("p (a b) -> p a b", a=3)
        fb = fcb[:, l * 44:(l + 1) * 44]
        def up_pair(i):
            slot = i % 8
            if True:
                wb, tok = wtile(ffn_up_srcs(l, i), 2048)
                for ti, (h0, hn, nout) in enumerate(CT[:3] + [(1530, 526, 524)]):
                    merged = hn > 512
                    if merged:
                        bg, bv = nb2(), nb2()
                    else:
                        bg, bv = nb(), nb()

                    def mm(e, wb=wb, h0=h0, hn=hn, bg=bg, bv=bv, merged=merged):
                        ins = None
                        for (bb_, wo) in ((bg, 0), (bv, 1024)):
                            for c in range(8):
                                lt = wb[:, wo + c * 128:wo + (c + 1) * 128]
                                ins = e.matmul(bank(bb_, min(hn, 512)), lhsT=lt, rhs=H[:, c, h0:h0 + min(hn, 512)],
                                               start=(c == 0), stop=(c == 7))
                                if merged:
                                    ins = e.matmul(bank(bb_ + 1, hn - 512), lhsT=lt, rhs=H[:, c, h0 + 512:h0 + hn],
                                                   start=(c == 0), stop=(c == 7))
                        return ins
                    pall = (lambda bb_: [("ps", bb_), ("ps", bb_ + 1)]) if merged else (lambda bb_: [("ps", bb_)])
                    S.op("pe", mm, [tok] + blks("H", range(8), h0, h0 + hn), pall(bg) + pall(bv))
                    sl = uid() % 2
                    jg, jv = i, NFC + i
                    for (bb, dst, j, nm) in ((bg, CG[sl], jg, "cg"), (bv, CV[sl], jv, "cv")):
                        S.op("act", lambda e, bb=bb, dst=dst, j=j, nout=nout: e.activation(
                            out=dst[:, 0:nout], in_=PS[:, bb * 512 + 1:bb * 512 + 1 + nout], func=AF.Identity,
                            bias=fb[:, j:j + 1], scale=fw[:, 1, j:j + 1]),
                            pall(bb) + ["fcw", "fcb"], [(nm, sl)])
                        S.op("dve", lambda e, bb=bb, dst=dst, j=j, nout=nout: e.scalar_tensor_tensor(
                            out=dst[:, 0:nout], in0=PS[:, bb * 512:bb * 512 + nout], scalar=fw[:, 0, j:j + 1],
                            in1=dst[:, 0:nout], op0=ALU.mult, op1=ALU.add),
                            pall(bb) + [(nm, sl), "fcw"], [(nm, sl)])
                        S.op("dve", lambda e, bb=bb, dst=dst, j=j, nout=nout: e.scalar_tensor_tensor(
                            out=dst[:, 0:nout], in0=PS[:, bb * 512 + 2:bb * 512 + 2 + nout], scalar=fw[:, 2, j:j + 1],
                            in1=dst[:, 0:nout], op0=ALU.mult, op1=ALU.add),
                            pall(bb) + [(nm, sl), "fcw"], [(nm, sl)])
                    S.op("act", lambda e, sl=sl, nout=nout: e.activation(out=SG[sl][:, 0:nout], in_=CG[sl][:, 0:nout],
                                                                          func=AF.Silu),
                         [("cg", sl)], [("sg", sl)])
                    S.op("pool", lambda e, sl=sl, nout=nout, slot=slot, h0=h0: e.tensor_tensor(
                        out=BIG[:, slot, h0:h0 + nout], in0=SG[sl][:, 0:nout], in1=CV[sl][:, 0:nout], op=ALU.mult),
                        [("sg", sl), ("cv", sl)], blk("BIG", slot, h0, h0 + nout))
        def down(grp, tiles=None):
            nk = len(grp)
            k0 = grp[0]
            linear_to_X(
                lambda o, k0=k0, nk=nk: ffn_down_src(l, k0, nk, o),
                nk, lambda k, t0, tn, k0=k0: BIG[:, (k0 + k) % 8, t0:t0 + tn],
                lambda t0, tn, nk=nk, k0=k0: blks("BIG", [(k0 + k) % 8 for k in range(nk)], t0, t0 + tn),
                tiles=tiles)

        for i in GROUPS[0]:
            up_pair(i)
        for g in range(1, len(GROUPS)):
            for i in GROUPS[g][:2]:
                up_pair(i)
            down(GROUPS[g - 1])
            for i in GROUPS[g][2:]:
                up_pair(i)
        down(GROUPS[-1], TT[:2])
        mask_left()
        down(GROUPS[-1], TT[2:])
        mask_right()

    MM_, TMP, CC = CG, CV, SG

    def mixer_c():
        norm_to_H(0)
        cw = ccw[:, :].rearrange("p (a b) -> p a b", a=3)
        for j in range(8):
            wb, tok = wtile(mixc_srcs(j), 3072)
            for (h0, hn, nout) in CT:
                n_ = uid()
                bb_, bc, bz = (0, 1, 2)[n_ % 3], (3, 4)[n_ % 2], (5, 6, 7)[n_ % 3]

                def mm(e, wb=wb, h0=h0, hn=hn, nout=nout, bb_=bb_, bc=bc, bz=bz):
                    ins = None
                    for c in range(8):
                        e.matmul(bank(bc, hn), lhsT=wb[:, 1024 + c * 128:1024 + (c + 1) * 128],
                                 rhs=H[:, c, h0:h0 + hn], start=(c == 0), stop=(c == 7))
                    for c in range(8):
                        e.matmul(bank(bz, hn), lhsT=wb[:, 2048 + c * 128:2048 + (c + 1) * 128],
                                 rhs=H[:, c, h0:h0 + hn], start=(c == 0), stop=(c == 7))
                    for c in range(8):
                        ins = e.matmul(bank(bb_, nout), lhsT=wb[:, c * 128:(c + 1) * 128],
                                       rhs=H[:, c, h0 + 1:h0 + 1 + nout], start=(c == 0), stop=(c == 7))
                    return ins
                S.op("pe", mm, [tok] + blks("H", range(8), h0, h0 + hn), [("ps", bb_), ("ps", bc), ("ps", bz)])
                sl = n_ % 2
                S.op("act", lambda e, sl=sl, hn=hn, bc=bc: e.copy(out=TMP[sl][:, 0:hn], in_=bank(bc, hn)),
                     [("ps", bc)], [("cv", sl)])
                S.op("dve", lambda e, sl=sl, hn=hn, bz=bz: e.tensor_tensor(
                    out=MM_[sl][:, 0:hn], in0=bank(bz, hn), in1=TMP[sl][:, 0:hn], op=ALU.mult),
                    [("ps", bz), ("cv", sl)], [("cg", sl)])
                S.op("act", lambda e, sl=sl, nout=nout, j=j: e.activation(
                    out=CC[sl][:, 0:nout], in_=MM_[sl][:, 1:1 + nout], func=AF.Identity,
                    bias=ccb[:, j:j + 1], scale=cw[:, 1, j:j + 1]), [("cg", sl), "ccw", "ccb"], [("sg", sl)])
                S.op("dve", lambda e, sl=sl, nout=nout, j=j: e.scalar_tensor_tensor(
                    out=CC[sl][:, 0:nout], in0=MM_[sl][:, 0:nout], scalar=cw[:, 0, j:j + 1],
                    in1=CC[sl][:, 0:nout], op0=ALU.mult, op1=ALU.add), [("cg", sl), ("sg", sl), "ccw"], [("sg", sl)])
                S.op("dve", lambda e, sl=sl, nout=nout, j=j: e.scalar_tensor_tensor(
                    out=CC[sl][:, 0:nout], in0=MM_[sl][:, 2:2 + nout], scalar=cw[:, 2, j:j + 1],
                    in1=CC[sl][:, 0:nout], op0=ALU.mult, op1=ALU.add), [("cg", sl), ("sg", sl), "ccw"], [("sg", sl)])
                S.op("dve", lambda e, sl=sl, nout=nout, j=j, h0=h0, bb_=bb_: e.tensor_tensor(
                    out=BIG[:, j, h0:h0 + nout], in0=bank(bb_, nout), in1=CC[sl][:, 0:nout], op=ALU.mult),
                    [("ps", bb_), ("sg", sl)], blk("BIG", j, h0, h0 + nout))
        for tiles, mfn in ((TT[:2], mask_left), (TT[2:], mask_right)):
            linear_to_X(lambda o: w_out_c[:, o * 128:(o + 1) * 128].rearrange("(k p) o -> p k o", p=128),
                        8, lambda k, t0, tn: BIG[:, k, t0:t0 + tn],
                        lambda t0, tn: blks("BIG", range(8), t0, t0 + tn), tiles=tiles)
            mfn()

    def final_out(slab):
        S.barrier()
        YTs = [YT, scr_f32(6144, 1024)]
        OSTs = [OST[0], scr_f32(8192, 1024)]
        rs_of = {}

        def stage1(i):
            ti, qq = divmod(i, 4)
            t0 = HALO + ti * 512
            if i == 0:
                rs_of[0] = rms_tile(t0, 512)
            if qq == 1 and ti + 1 < 4:
                rs_of[ti + 1] = rms_tile(t0 + 512, 512)
            RSp, rtok = rs_of[ti]
            c0 = t0 + qq * 128
            yt = YTs[i % 2]
            for c in range(8):
                S.op("dve", lambda e, c=c, c0=c0, qq=qq, RSp=RSp, yt=yt: e.scalar_tensor_tensor(
                    out=yt[:, c * 128:(c + 1) * 128], in0=X[:, c, c0:c0 + 128],
                    scalar=gvec[:, 24 + c:25 + c], in1=RSp[:, qq * 128:(qq + 1) * 128],
                    op0=ALU.mult, op1=ALU.mult),
                    blk("X", c, c0, c0 + 128) + [rtok, "gvec"], [("yt", i % 2, c)])

        def stage2(i):
            yt, ost = YTs[i % 2], OSTs[i % 2]
            b2 = nb2()

            def tr(e, b2=b2, yt=yt):
                ins = None
                for c in range(8):
                    ins = e.transpose(out=PS[:, b2 * 512 + c * 128:b2 * 512 + (c + 1) * 128],
                                      in_=yt[:, c * 128:(c + 1) * 128], identity=identf[:, :])
                return ins
            S.op("pe", tr, [("yt", i % 2, c) for c in range(8)] + ["identf"], [("ps", b2), ("ps", b2 + 1)])
            S.op("act", lambda e, b2=b2, ost=ost: e.copy(out=ost, in_=PS[:, b2 * 512:b2 * 512 + 1024]),
                 [("ps", b2), ("ps", b2 + 1)], [("ost", i % 2)])
            r0 = i * 128
            S.dma("pool", lambda e, r0=r0, ost=ost: e.dma_start(out=yout[slab][r0:r0 + 128, :], in_=ost),
                  [("ost", i % 2)], [("yout", slab, r0)], ("ost", i % 2))

        for i in range(17):
            if i < 16:
                stage1(i)
            if i >= 1:
                stage2(i - 1)
        S.barrier()

    for slab in ("s", "p"):
        S.barrier()
        S.dma("sp", lambda e, slab=slab: e.dma_start(out=maskt[:, :], in_=maskd[slab].rearrange("p c m -> p (c m)")),
              [], ["mask"], ("c", "mask"))
        f_pass(slab, 128 if slab == "s" else 16)
        S.barrier()
        dft(slab)
        S.barrier()
        front(slab)
        S.barrier()
        for tiles, mfn in ((TT[:2], mask_left), (TT[2:], mask_right)):
            linear_to_X(lambda o: w_out_ab[:, o * 128:(o + 1) * 128].rearrange("(k p) o -> p k o", p=128),
                        8, lambda k, t0, tn: BIG[:, k, t0:t0 + tn],
                        lambda t0, tn: blks("BIG", range(8), t0, t0 + tn), tiles=tiles)
            mfn()
        ffn(0)
        mixer_c()
        ffn(1)
        final_out(slab)
    S.emit(nc)
    return nc


def _tables():
    t = {}
    t["identf"] = np.eye(128, dtype=np.float32)
    t["identb"] = np.eye(128, dtype=np.float32).astype(NPBF)
    c = np.arange(128)
    ang = 2 * np.pi * np.outer(c, c) / 128.0
    cd = np.stack([np.cos(ang), np.sin(ang)], axis=1) / np.sqrt(128.0)
    t["CD"] = cd.astype(np.float32).astype(NPBF)
    for slab, n in (("p", 16), ("s", 128)):
        s2 = np.arange(n)
        a = 2 * np.pi * np.outer(s2, s2) / n
        t["WA_" + slab] = np.concatenate([np.cos(a), -np.sin(a)], axis=1).astype(np.float32).astype(NPBF)
    return t


def _mt_prompt():
    s1 = np.arange(128, dtype=np.float64)[:, None, None]
    s2 = np.arange(16, dtype=np.float64)[None, :, None]
    out = np.zeros((4, 2, 128, 16, 512), dtype=np.float32)
    sc = 1.0 / np.sqrt(float(S_P))
    for kt in range(4):
        k = (kt * 512 + np.arange(512, dtype=np.float64))[None, None, :]
        ang = 2 * np.pi * np.mod((s2 * 128 + s1) * k, S_P) / S_P
        out[kt, 0] = np.cos(ang) * sc
        out[kt, 1] = -np.sin(ang) * sc
    return out.reshape(4, 2, 128, 16 * 512).astype(NPBF)


def _mb(S_len, n, k1_list):
    s1 = np.arange(128, dtype=np.float64)[:, None, None]
    k2 = np.arange(n, dtype=np.float64)[None, :, None]
    k1 = np.asarray(k1_list, dtype=np.float64)[None, None, :]
    k = np.mod(n * k1 + k2, S_len)
    ang = 2 * np.pi * np.mod(s1 * k, S_len) / S_len
    sc = 1.0 / np.sqrt(S_len)
    mr = np.cos(ang) * sc
    mi = -np.sin(ang) * sc
    mb1 = np.concatenate([mr, mi], axis=2)
    mb2 = np.concatenate([-mi, mr], axis=2)
    return mb1.astype(np.float32).astype(NPBF), mb2.astype(np.float32).astype(NPBF)


_CACHE = {}


def kernel(x_prompt, x_sample, norm_mix, w_in_ab, sgu_gain, w_s, b_s, w_out_ab,
           w_in_c, conv_w_c, conv_b_c, w_out_c, norm_ffn, w_up, ffn_conv_w, ffn_conv_b, w_down, final_norm):
    f32 = np.float32
    A = lambda a: np.ascontiguousarray(np.asarray(a, dtype=f32))
    x_prompt, x_sample = A(x_prompt), A(x_sample)
    if "nc" not in _CACHE:
        _CACHE["nc"] = build_program()
        _CACHE["tab"] = _tables()
    nc = _CACHE["nc"]
    tab = _CACHE["tab"]
    shared = {
        "w_in_ab": A(w_in_ab)[0], "w_out_ab": A(w_out_ab)[0], "w_in_c": A(w_in_c)[0], "w_out_c": A(w_out_c)[0],
        "w_up": A(w_up), "w_down": A(w_down),
        "g0bc": np.ascontiguousarray(np.broadcast_to(A(norm_mix)[0][None, :], (128, D))),
        "gainbc": np.ascontiguousarray(np.broadcast_to(A(sgu_gain)[0].reshape(1, 512), (128, 512))),
        "bst": np.ascontiguousarray(A(b_s)[0].T),
        "wsT": np.ascontiguousarray(A(w_s)[0].transpose(2, 0, 1)),
        "gvec": np.ascontiguousarray(np.stack([A(norm_mix)[1], A(norm_ffn)[0], A(norm_ffn)[1], A(final_norm)], 0)
                                     .reshape(4, 8, 128).transpose(2, 0, 1)),
        "ccw": np.ascontiguousarray(A(conv_w_c)[0].reshape(3, 8, 128).transpose(2, 0, 1)),
        "ccb": np.ascontiguousarray(A(conv_b_c)[0].reshape(8, 128).T),
        "fcw": np.ascontiguousarray(A(ffn_conv_w).reshape(2, 3, 44, 128).transpose(3, 0, 1, 2)),
        "fcb": np.ascontiguousarray(A(ffn_conv_b).reshape(2, 44, 128).transpose(2, 0, 1)),
        "identf": tab["identf"], "identb": tab["identb"], "CD": tab["CD"],
        "WA_p": tab["WA_p"], "WA_s": tab["WA_s"],
        "xa_s": x_sample[0],
    }
    mb1p, mb2p = _mb(S_P, 16, list(range(128)))
    shared["MB1_p"], shared["MB2_p"] = mb1p, mb2p
    if "mt" not in _CACHE:
        _CACHE["mt"] = _mt_prompt()
    shared["MT_p"] = _CACHE["mt"]
    in_maps = []
    for j in range(NCORE):
        m = dict(shared)
        m["xa_p"] = x_prompt[j]
        xe_p = np.zeros((18 * 128, D), f32)
        xe_p[128:128 + S_P] = x_prompt[j]
        m["xe_p"] = xe_p
        xe_s = np.zeros((18 * 128, D), f32)
        lo = 2048 * j - 128
        hi = lo + 18 * 128
        slo, shi = max(lo, 0), min(hi, S_S)
        xe_s[slo - lo:shi - lo] = x_sample[0, slo:shi]
        m["xe_s"] = xe_s
        m["mask_p"] = np.zeros((128, 8, 6), f32)
        ms = np.ones((128, 8, 6), f32)
        if j == 0:
            ms[:, :, 0:3] = 0
        if j == NCORE - 1:
            ms[:, :, 3:6] = 0
        m["mask_s"] = ms
        mb1, mb2 = _mb(S_S, 128, [16 * j - 1 + i for i in range(18)])
        m["MB1_s"], m["MB2_s"] = mb1, mb2
        in_maps.append(m)
    res = run_bass_kernel_spmd(nc, in_maps, core_ids=list(range(NCORE)))
    yp = np.stack([np.asarray(res.results[j]["y_p"], dtype=f32) for j in range(NCORE)], 0)
    ys = np.concatenate([np.asarray(res.results[j]["y_s"], dtype=f32) for j in range(NCORE)], 0)[None]
    return yp, ys
```

```python
import numpy as np
import ml_dtypes
import concourse.bass as bass
import concourse.mybir as mybir
from concourse.bass_utils import run_bass_kernel_spmd

F32 = mybir.dt.float32
BF16 = mybir.dt.bfloat16
I32 = mybir.dt.int32
AF = mybir.ActivationFunctionType
ALU = mybir.AluOpType
NPBF = ml_dtypes.bfloat16

D = 1024
S_P = 2048
S_S = 16384
NCORE = 8
W = 2054
HALO = 3
E0 = 125
DFF = 2816
NFC = 22
EPS = 1e-6
TT = [(0, 512), (512, 512), (1024, 512), (1536, 512), (2048, 6)]
CT = [(510 * n, 512, 510) for n in range(4)] + [(2040, 16, 14)]
GROUPS = [list(range(0, 6)), list(range(6, 12)), list(range(12, 17)), list(range(17, 22))]


class Op:
    __slots__ = ("eng", "fn", "deps", "sig", "sigval", "kind", "key", "dval")


class Sched:
    ENGS = ["sp", "act", "dve", "pool", "pe"]

    def __init__(self):
        self.ops = {e: [] for e in self.ENGS}
        self.lastw = {}
        self.readers = {}
        self.bar = []
        self.keycnt = {}
        self.lastdma = {}

    def _mk(self, eng, fn, R, Wr):
        o = Op()
        o.eng, o.fn, o.sig, o.sigval, o.kind, o.key, o.dval = eng, fn, False, 0, "c", None, 0
        deps = set(self.bar)
        for r in R:
            w = self.lastw.get(r)
            if w is not None:
                deps.add(w)
        for r in Wr:
            w = self.lastw.get(r)
            if w is not None:
                deps.add(w)
            for rd in self.readers.get(r, ()):
                deps.add(rd)
        for r in R:
            self.readers.setdefault(r, []).append(o)
        for r in Wr:
            self.lastw[r] = o
            self.readers[r] = []
        deps.discard(o)
        o.deps = deps
        for d in deps:
            if d.kind == "c":
                d.sig = True
        self.ops[eng].append(o)
        return o

    def op(self, eng, fn, R=(), Wr=()):
        return self._mk(eng, fn, R, Wr)

    def dma(self, eng, fn, R, Wr, key):
        o = self._mk(eng, fn, R, Wr)
        o.kind = "d"
        o.key = key
        self.keycnt[key] = self.keycnt.get(key, 0) + 1
        o.dval = 16 * self.keycnt[key]
        self.lastdma[key] = o
        return o

    def barrier(self):
        b = []
        for e in self.ENGS:
            for o in reversed(self.ops[e]):
                if o.kind == "c":
                    o.sig = True
                    b.append(o)
                    break
        b.extend(self.lastdma.values())
        self.bar = b

    def emit(self, nc):
        sems = {e: nc.alloc_semaphore("s_" + e) for e in self.ENGS}
        dsem = {k: nc.alloc_semaphore("d_%d" % i) for i, k in enumerate(self.keycnt)}
        for e in self.ENGS:
            cnt = 0
            for o in self.ops[e]:
                if o.kind == "c" and o.sig:
                    cnt += 1
                    o.sigval = cnt
        fin = []
        for e in self.ENGS:
            for o in reversed(self.ops[e]):
                if o.kind == "c" and o.sig:
                    fin.append((sems[e], o.sigval))
                    break
        for k, o in self.lastdma.items():
            fin.append((dsem[k], o.dval))
        ops = self.ops

        def make(ename):
            def body(eng):
                waited = {}
                for o in ops[ename]:
                    need = {}
                    for d in o.deps:
                        if d.kind == "c":
                            if d.eng == "pe" and ename == "pe":
                                continue
                            sem, val = sems[d.eng], d.sigval
                        else:
                            sem, val = dsem[d.key], d.dval
                        if waited.get(id(sem), 0) >= val:
                            continue
                        if need.get(id(sem), (None, 0))[1] < val:
                            need[id(sem)] = (sem, val)
                    for sid, (sem, val) in need.items():
                        eng.wait_ge(sem, val)
                        waited[sid] = val
                    ins = o.fn(eng)
                    if o.kind == "c":
                        if o.sig:
                            ins.then_inc(sems[ename], 1)
                    else:
                        ins.then_inc(dsem[o.key], 16)
                if ename == "sp":
                    for sem, val in fin:
                        eng.wait_ge(sem, val)
            return body

        with nc.Block() as block:
            block.sync(make("sp"))
            block.scalar(make("act"))
            block.vector(make("dve"))
            block.gpsimd(make("pool"))
            block.tensor(make("pe"))


def blk(name, c, lo, hi):
    return [(name, c, b) for b in range(lo // 512, (hi - 1) // 512 + 1)]


def blks(name, cs, lo, hi):
    r = []
    for c in cs:
        r += blk(name, c, lo, hi)
    return r


def build_program():
    nc = bass.Bass("TRN2", target_bir_lowering=False)
    S = Sched()

    def din(name, shape, dt=F32):
        return nc.dram_tensor(name, list(shape), dt, kind="ExternalInput").ap()

    xa = {"p": din("xa_p", [S_P, D]), "s": din("xa_s", [S_S, D])}
    xe = {"p": din("xe_p", [18 * 128, D]), "s": din("xe_s", [18 * 128, D])}
    maskd = {"p": din("mask_p", [128, 8, 6]), "s": din("mask_s", [128, 8, 6])}
    w_in_ab = din("w_in_ab", [D, 1536])
    w_out_ab = din("w_out_ab", [D, D])
    w_in_c = din("w_in_c", [D, 3072])
    w_out_c = din("w_out_c", [D, D])
    w_up = din("w_up", [2, D, 2 * DFF])
    w_down = din("w_down", [2, DFF, D])
    g0bc_d = din("g0bc", [128, D])
    gainbc_d = din("gainbc", [128, 512])
    bst_d = din("bst", [128, 4])
    wsT_d = din("wsT", [128, 4, 128])
    gvec_d = din("gvec", [128, 4, 8])
    ccw_d = din("ccw", [128, 3, 8])
    ccb_d = din("ccb", [128, 8])
    fcw_d = din("fcw", [128, 2, 3, 44])
    fcb_d = din("fcb", [128, 2, 44])
    identf_d = din("identf", [128, 128])
    identb_d = din("identb", [128, 128], BF16)
    WA_d = {"p": din("WA_p", [16, 32], BF16), "s": din("WA_s", [128, 256], BF16)}
    MB1_d = {"p": din("MB1_p", [128, 16, 256], BF16), "s": din("MB1_s", [128, 128, 36], BF16)}
    MB2_d = {"p": din("MB2_p", [128, 16, 256], BF16), "s": din("MB2_s", [128, 128, 36], BF16)}
    CD_d = din("CD", [128, 2, 128], BF16)
    MT_d = din("MT_p", [4, 2, 128, 16 * 512], BF16)
    yout = {"p": nc.dram_tensor("y_p", [S_P, D], F32, kind="ExternalOutput").ap(),
            "s": nc.dram_tensor("y_s", [S_P, D], F32, kind="ExternalOutput").ap()}
    Fd = {"p": nc.dram_tensor("F_p", [4, S_P, 128], BF16).ap(),
          "s": nc.dram_tensor("F_s", [4, S_S, 128], BF16).ap()}

    XR = nc.alloc_sbuf_tensor("XR", [128, 8 * W], F32)
    HR = nc.alloc_sbuf_tensor("HR", [128, 8 * 2056], BF16)
    BR = nc.alloc_sbuf_tensor("BR", [128, 8 * W], BF16)
    WST = [nc.alloc_sbuf_tensor("WST%d" % i, [128, 3072], F32) for i in range(2)]
    WBF = [nc.alloc_sbuf_tensor("WBF%d" % i, [128, 3072], BF16) for i in range(2)]
    SCR = nc.alloc_sbuf_tensor("SCR", [128, 17408], BF16)
    PS = nc.alloc_psum_tensor("PS", [128, 8 * 512], F32)
    bst = nc.alloc_sbuf_tensor("bst_s", [128, 4], F32)
    wsT = nc.alloc_sbuf_tensor("wsT_s", [128, 512], BF16)
    gvec = nc.alloc_sbuf_tensor("gvec_s", [128, 32], F32)
    ccw = nc.alloc_sbuf_tensor("ccw_s", [128, 24], F32)
    ccb = nc.alloc_sbuf_tensor("ccb_s", [128, 8], F32)
    fcw = nc.alloc_sbuf_tensor("fcw_s", [128, 264], F32)
    fcb = nc.alloc_sbuf_tensor("fcb_s", [128, 88], F32)
    identf = nc.alloc_sbuf_tensor("identf_s", [128, 128], F32)
    identb = nc.alloc_sbuf_tensor("identb_s", [128, 128], BF16)
    onesb = nc.alloc_sbuf_tensor("onesb", [128, 128], BF16)
    epsT = nc.alloc_sbuf_tensor("epsT", [128, 1], F32)
    maskt = nc.alloc_sbuf_tensor("maskt", [128, 48], F32)
    CDt = nc.alloc_sbuf_tensor("CDt", [128, 256], BF16)
    WAt = nc.alloc_sbuf_tensor("WAt", [128, 256], BF16)
    stat = nc.alloc_sbuf_tensor("stat", [128, 64], F32)

    X = XR[:, :].rearrange("p (c n) -> p c n", c=8)
    H = HR[:, :].rearrange("p (c n) -> p c n", c=8)
    BIG = BR[:, :].rearrange("p (c n) -> p c n", c=8)

    def bank(b, n=512):
        return PS[:, b * 512:b * 512 + n]

    bstate = {"b": 0}

    def nb():
        b = bstate["b"]
        bstate["b"] = (b + 1) % 8
        return b

    def nb2():
        if bstate["b"] % 2:
            bstate["b"] = (bstate["b"] + 1) % 8
        b = bstate["b"]
        bstate["b"] = (b + 2) % 8
        return b

    cnt = {"k": 0}

    def uid():
        cnt["k"] += 1
        return cnt["k"]

    def ld(dst, src, name):
        S.dma("sp", lambda e, d=dst, s=src: e.dma_start(out=d, in_=s), [], [name], ("c", name))

    ld(bst[:, :], bst_d[:, :], "bst")
    ld(WST[0][:, 0:512], wsT_d.rearrange("q h p -> q (h p)"), "wsTf")
    ld(gvec[:, :], gvec_d.rearrange("p a b -> p (a b)"), "gvec")
    ld(ccw[:, :], ccw_d.rearrange("p a b -> p (a b)"), "ccw")
    ld(ccb[:, :], ccb_d[:, :], "ccb")
    ld(fcw[:, :], fcw_d.rearrange("p l a b -> p (l a b)"), "fcw")
    ld(fcb[:, :], fcb_d.rearrange("p l b -> p (l b)"), "fcb")
    ld(identf[:, :], identf_d[:, :], "identf")
    ld(identb[:, :], identb_d[:, :], "identb")
    ld(CDt[:, :], CD_d.rearrange("p a b -> p (a b)"), "CD")
    S.op("pool", lambda e: e.memset(onesb[:, :], 1.0), [], ["ones"])
    S.op("pool", lambda e: e.memset(epsT[:, :], EPS), [], ["eps"])
    S.op("pool", lambda e: e.memset(BR[:, :], 0.0), [], ["BRz"])
    S.op("pool", lambda e: e.memset(HR[:, :], 0.0), [], ["HRz"])
    S.op("dve", lambda e: e.tensor_copy(out=wsT[:, :], in_=WST[0][:, 0:512]), ["wsTf"], ["wsT"])
    CONSTS = ["g0bc", "gainbc", "bst", "wsT", "gvec", "ccw", "ccb", "fcw", "fcb", "identf", "identb", "CD",
              "ones", "eps", "BRz", "HRz"]

    wplan = []
    wstate = {"i": 0, "issued": 0}

    def _issue(k):
        pieces, ncols = wplan[k]
        s = k % 2
        off = 0
        for pi, src in enumerate(pieces):
            a, b = src.shape[1], src.shape[2]
            dst = WST[s][:, off:off + a * b].rearrange("p (a b) -> p a b", a=a)
            S.dma("sp", lambda e, d=dst, sr=src: e.dma_start(out=d, in_=sr), [], [("wst", s, pi)], ("wst", s, pi))
            off += a * b
        assert off == ncols and ncols <= 3072
        S.op("act", lambda e, s=s, n=ncols: e.copy(out=WBF[s][:, 0:n], in_=WST[s][:, 0:n]),
             [("wst", s, pi) for pi in range(3)], [("wbf", s)] + [("wst", s, pi) for pi in range(3)])

    def wtile(pieces, ncols):
        k = wstate["i"]
        wstate["i"] += 1
        assert wplan[k][1] == ncols and len(wplan[k][0]) == len(pieces), (k, wplan[k][1], ncols)
        while wstate["issued"] <= min(k + 1, len(wplan) - 1):
            _issue(wstate["issued"])
            wstate["issued"] += 1
        return WBF[k % 2], ("wbf", k % 2)

    def wprefetch_extra():
        if wstate["issued"] < len(wplan):
            _issue(wstate["issued"])
            wstate["issued"] += 1

    def win_srcs(lo, n):
        return [([w_in_ab[:, lo + p0:lo + p0 + min(384, n - p0)].rearrange("(c p) o -> p c o", p=128)],
                 8 * min(384, n - p0)) for p0 in range(0, n, 384)]

    def wout_ab_src(o):
        return w_out_ab[:, o * 128:(o + 1) * 128].rearrange("(k p) o -> p k o", p=128)

    def wout_c_src(o):
        return w_out_c[:, o * 128:(o + 1) * 128].rearrange("(k p) o -> p k o", p=128)

    def ffn_up_srcs(l, i):
        return [w_up[l][:, i * 128:(i + 1) * 128].rearrange("(c p) o -> p c o", p=128),
                w_up[l][:, DFF + i * 128:DFF + (i + 1) * 128].rearrange("(c p) o -> p c o", p=128)]

    def ffn_down_src(l, k0, nk, o):
        return w_down[l][k0 * 128:(k0 + nk) * 128, o * 128:(o + 1) * 128].rearrange("(k p) o -> p k o", p=128)

    def mixc_srcs(j):
        return [w_in_c[:, kk * 1024 + j * 128:kk * 1024 + (j + 1) * 128].rearrange("(c p) o -> p c o", p=128)
                for kk in range(3)]

    def plan_ffn(l):
        up = lambda i: (ffn_up_srcs(l, i), 2048)
        dn = lambda grp: [([ffn_down_src(l, grp[0], len(grp), o)], len(grp) * 128) for o in range(8)]
        r = [up(i) for i in GROUPS[0]]
        for g in range(1, len(GROUPS)):
            r += [up(i) for i in GROUPS[g][:2]]
            r += dn(GROUPS[g - 1])
            r += [up(i) for i in GROUPS[g][2:]]
        r += dn(GROUPS[-1])
        r += dn(GROUPS[-1])
        return r

    for _slab in ("s", "p"):
        wplan += win_srcs(1024, 512)
        wplan += win_srcs(0, 1024)
        wplan += [([wout_ab_src(o)], 1024) for o in range(8)] * 2
        wplan += plan_ffn(0)
        wplan += [(mixc_srcs(j), 3072) for j in range(8)]
        wplan += [([wout_c_src(o)], 1024) for o in range(8)] * 2
        wplan += plan_ffn(1)

    def scr_f32(off, n):
        return SCR[:, off:off + 2 * n].bitcast(F32)

    FP_XT = [scr_f32(s_ * 4608, 1024) for s_ in range(3)]
    FP_XN = [SCR[:, s_ * 4608 + 2048:s_ * 4608 + 3072] for s_ in range(3)]
    FP_HT = [SCR[:, s_ * 4608 + 3072:s_ * 4608 + 4096] for s_ in range(3)]
    FB = [SCR[:, s_ * 4608 + 4096:s_ * 4608 + 4608] for s_ in range(3)]
    FR_XT = [scr_f32(s_ * 2048, 1024) for s_ in range(3)]
    FR_XN = [SCR[:, 6144 + s_ * 1024:6144 + (s_ + 1) * 1024] for s_ in range(3)]
    FR_HT = [SCR[:, 9216 + s_ * 1024:9216 + (s_ + 1) * 1024] for s_ in range(2)]
    VV2 = [scr_f32(11264 + s_ * 1024, 512) for s_ in range(2)]
    VN3 = [SCR[:, 13312:13824], SCR[:, 13824:14336], HR[:, 15360:15872]]
    UU3 = [HR[:, 12288 + s_ * 1024:12288 + (s_ + 1) * 1024].bitcast(F32) for s_ in range(3)]
    AA1 = HR[:, 15872:16384]
    g0bc = scr_f32(14336, 1024)
    gainbc = scr_f32(16384, 512)

    def load_gconsts():
        ld(g0bc, g0bc_d[:, :], "g0bc")
        ld(gainbc, gainbc_d[:, :], "gainbc")
    WIN = HR[:, 0:8 * 1536].rearrange("p (c n) -> p c n", c=8)

    def load_win(cols_lo, cols_n):
        toks = []
        for (pieces, ncols), p0 in zip(win_srcs(cols_lo, cols_n), range(0, cols_n, 384)):
            pn = ncols // 8
            wb, tok = wtile(pieces, ncols)
            S.op("act", lambda e, wb=wb, p0=p0, pn=pn: e.copy(
                out=WIN[:, :, p0:p0 + pn], in_=wb[:, 0:8 * pn].rearrange("p (c n) -> p c n", c=8)),
                [tok], [("win", p0)])
            toks.append(("win", p0))
        return toks

    def newton_rsqrt(a, b, c, n, tin, ty, tt_):
        xs, ys, ts = stat[:, a:a + n], stat[:, b:b + n], stat[:, c:c + n]
        xi, yi = xs.bitcast(I32), ys.bitcast(I32)
        S.op("dve", lambda e: e.tensor_scalar(out=xs, in0=xs, scalar1=EPS, scalar2=None, op0=ALU.add), [tin], [tin])
        S.op("dve", lambda e: e.tensor_scalar(out=yi, in0=xi, scalar1=1, scalar2=None, op0=ALU.arith_shift_right),
             [tin], [ty])
        S.op("dve", lambda e: e.tensor_scalar(out=yi, in0=yi, scalar1=-1, scalar2=0x5f3759df, op0=ALU.mult,
                                              op1=ALU.add), [ty], [ty])
        for _ in range(2):
            S.op("dve", lambda e: e.tensor_tensor(out=ts, in0=ys, in1=ys, op=ALU.mult), [ty], [tt_])
            S.op("dve", lambda e: e.scalar_tensor_tensor(out=ts, in0=ts, scalar=-0.5, in1=xs, op0=ALU.mult,
                                                         op1=ALU.mult), [tt_, tin], [tt_])
            S.op("dve", lambda e: e.scalar_tensor_tensor(out=ys, in0=ts, scalar=1.5, in1=ys, op0=ALU.add,
                                                         op1=ALU.mult), [ty, tt_], [ty])

    def chunk_A(src_rows, q, XT, XN, newton=False):
        sl = q % len(XT)
        S.dma("sp", lambda e, sl=sl, sr=src_rows: e.dma_start(out=XT[sl], in_=sr), [], [("xt", sl)], ("xt", sl))
        S.op("act", lambda e, sl=sl: e.activation(out=XN[sl], in_=XT[sl], func=AF.Square, scale=float(D ** -0.5),
                                                   accum_out=stat[:, 3 * sl:3 * sl + 1]),
             [("xt", sl)], [("xn", sl), ("ms", sl)])
        if newton:
            newton_rsqrt(3 * sl, 3 * sl + 2, 3 * sl + 1, 1, ("ms", sl), ("rs", sl), ("sd", sl))
        else:
            S.op("act", lambda e, sl=sl: e.activation(out=stat[:, 3 * sl + 1:3 * sl + 2],
                                                       in_=stat[:, 3 * sl:3 * sl + 1],
                                                       func=AF.Sqrt, bias=epsT[:, 0:1], scale=1.0),
                 [("ms", sl), "eps"], [("sd", sl)])
            S.op("dve", lambda e, sl=sl: e.reciprocal(out=stat[:, 3 * sl + 2:3 * sl + 3],
                                                      in_=stat[:, 3 * sl + 1:3 * sl + 2]),
                 [("sd", sl)], [("rs", sl)])
        S.op("dve", lambda e, sl=sl: e.scalar_tensor_tensor(out=XN[sl], in0=XT[sl],
                                                            scalar=stat[:, 3 * sl + 2:3 * sl + 3],
                                                            in1=g0bc, op0=ALU.mult, op1=ALU.mult),
             [("xt", sl), ("rs", sl), "g0bc"], [("xn", sl)])

    def chunk_B(q, XN, HT, b=None):
        sl = q % len(XN)
        if b is None:
            b = nb()
        pb = bank(b).bitcast(BF16)

        def tr(e, sl=sl, pb=pb):
            ins = None
            for c in range(8):
                ins = e.transpose(out=pb[:, c * 128:(c + 1) * 128], in_=XN[sl][:, c * 128:(c + 1) * 128],
                                  identity=identb[:, :])
            return ins
        S.op("pe", tr, [("xn", sl), "identb"], [("ps", b)])
        S.op("act", lambda e, sl=sl, pb=pb: e.copy(out=HT[sl], in_=pb[:, 0:1024]), [("ps", b)], [("ht", sl)])

    FPR = XR[:, 0:4096].bitcast(BF16).rearrange("p (s c) -> p s c", s=16)

    def f_pass(slab, nchunks):
        load_gconsts()
        wt = load_win(1024, 512)
        wprefetch_extra()

        def stage_C(q):
            sl = q % 3
            b = nb()

            def mm(e, sl=sl, b=b):
                ins = None
                for c in range(8):
                    ins = e.matmul(bank(b), lhsT=FP_HT[sl][:, c * 128:(c + 1) * 128], rhs=WIN[:, c, 0:512],
                                   start=(c == 0), stop=(c == 7))
                return ins
            S.op("pe", mm, [("ht", sl)] + wt, [("ps", b)])
            if slab == "p":
                S.op("dve", lambda e, q=q, b=b: e.tensor_copy(out=FPR[:, q, :], in_=bank(b)),
                     [("ps", b)], [("F", slab, q)])
                return
            S.op("dve", lambda e, sl=sl, b=b: e.tensor_copy(out=FB[sl], in_=bank(b)), [("ps", b)], [("fb", sl)])
            S.dma("pool", lambda e, sl=sl, q=q: e.dma_start(
                out=Fd[slab][:, q * 128:(q + 1) * 128, :].rearrange("g t c -> t g c"),
                in_=FB[sl].rearrange("p (g c) -> p g c", g=4)),
                [("fb", sl)], [("F", slab, q)], ("fst", sl))

        for t in range(nchunks + 2):
            if t < nchunks:
                chunk_A(xa[slab][t * 128:(t + 1) * 128, :], t, FP_XT, FP_XN)
            if 0 <= t - 1 < nchunks:
                chunk_B(t - 1, FP_XN, FP_HT)
            if 0 <= t - 2 < nchunks:
                stage_C(t - 2)

    def front(slab):
        load_gconsts()
        wt = load_win(0, 1024)
        NQ = 18

        def geom(q):
            lo = max(0, q * 128 - E0)
            hi = min(W, (q + 1) * 128 - E0)
            return lo, hi, lo + E0 - q * 128, hi - lo

        def sb(p):
            return 16 + p * 16

        def st_A1_dma(q):
            sl = q % 3
            S.dma("sp", lambda e, sl=sl, q=q: e.dma_start(out=FR_XT[sl], in_=xe[slab][q * 128:(q + 1) * 128, :]),
                  [], [("xt", sl)], ("xt", sl))

        def st_A1(q):
            sl = q % 3
            S.op("act", lambda e, sl=sl: e.activation(out=FR_XN[sl], in_=FR_XT[sl], func=AF.Square,
                                                       scale=float(D ** -0.5),
                                                       accum_out=stat[:, sb(q % 2) + 4:sb(q % 2) + 5]),
                 [("xt", sl)], [("xn", sl), ("msb", q % 2)])

        def st_A2(q):
            sl = q % 3
            yc = sb(q % 2) + 9
            S.op("dve", lambda e, sl=sl, yc=yc: e.scalar_tensor_tensor(out=FR_XN[sl], in0=FR_XT[sl],
                                                                       scalar=stat[:, yc:yc + 1],
                                                                       in1=g0bc, op0=ALU.mult, op1=ALU.mult),
                 [("xt", sl), ("yb", q % 2), "g0bc"], [("xn", sl)])

        def st_B(q):
            sl = q % 3
            hs = q % 2
            lo, hi, i0, n = geom(q)
            b = 0 if q % 2 == 0 else 7
            pb = bank(b).bitcast(BF16)

            def tr(e, sl=sl, pb=pb):
                ins = None
                for c in range(8):
                    ins = e.transpose(out=pb[:, c * 128:(c + 1) * 128], in_=FR_XN[sl][:, c * 128:(c + 1) * 128],
                                      identity=identb[:, :])
                return ins
            S.op("pe", tr, [("xn", sl), "identb"], [("ps", b)])
            S.op("act", lambda e, hs=hs, pb=pb: e.copy(out=FR_HT[hs], in_=pb[:, 0:1024]), [("ps", b)], [("ht", hs)])
            b2 = 1

            def trx(e, sl=sl, b2=b2):
                ins = None
                for c in range(8):
                    ins = e.transpose(out=PS[:, b2 * 512 + c * 128:b2 * 512 + (c + 1) * 128],
                                      in_=FR_XT[sl][:, c * 128:(c + 1) * 128], identity=identf[:, :])
                return ins
            S.op("pe", trx, [("xt", sl), "identf"], [("ps", b2), ("ps", b2 + 1)])
            S.op("act", lambda e, b2=b2, lo=lo, n=n, i0=i0: e.copy(
                out=X[:, :, lo:lo + n],
                in_=PS[:, b2 * 512:b2 * 512 + 1024].rearrange("p (c t) -> p c t", c=8)[:, :, i0:i0 + n]),
                [("ps", b2), ("ps", b2 + 1)], blks("X", range(8), lo, hi))

        def st_C1(q):
            hs, u3, v2 = q % 2, q % 3, q % 2
            UU, VV, VN = UU3[u3], VV2[v2], VN3[u3]
            so = sb((q + 3) % 2)
            bu, bv = 3, 4

            def mmu(e, hs=hs, bu=bu):
                ins = None
                for c in range(8):
                    ins = e.matmul(bank(bu), lhsT=FR_HT[hs][:, c * 128:(c + 1) * 128], rhs=WIN[:, c, 0:512],
                                   start=(c == 0), stop=(c == 7))
                return ins

            def mmv(e, hs=hs, bv=bv):
                ins = None
                for c in range(8):
                    ins = e.matmul(bank(bv), lhsT=FR_HT[hs][:, c * 128:(c + 1) * 128], rhs=WIN[:, c, 512:1024],
                                   start=(c == 0), stop=(c == 7))
                return ins
            S.op("pe", mmu, [("ht", hs)] + wt, [("ps", bu)])
            S.op("pe", mmv, [("ht", hs)] + wt, [("ps", bv)])
            S.op("act", lambda e, bu=bu, UU=UU: e.activation(out=UU, in_=bank(bu), func=AF.Gelu_apprx_tanh),
                 [("ps", bu)], [("uu", u3)])
            S.op("act", lambda e, bv=bv, VV=VV: e.activation(out=VV, in_=bank(bv), func=AF.Gelu_apprx_tanh),
                 [("ps", bv)], [("vv", v2)])
            for h in range(4):
                S.op("act", lambda e, h=h, VV=VV, VN=VN, so=so: e.activation(
                    out=VN[:, h * 128:(h + 1) * 128], in_=VV[:, h * 128:(h + 1) * 128],
                    func=AF.Square, scale=float(128 ** -0.5), accum_out=stat[:, so + h:so + h + 1]),
                    [("vv", v2)], [("vn", u3, h), ("msb", (q + 3) % 2)])

        def st_C2(q):
            u3, v2 = q % 3, q % 2
            VV, VN = VV2[v2], VN3[u3]
            pb_ = (q + 3) % 2
            so = sb(pb_)
            for h in range(4):
                S.op("dve", lambda e, h=h, VV=VV, VN=VN, so=so: e.scalar_tensor_tensor(
                    out=VN[:, h * 128:(h + 1) * 128], in0=VV[:, h * 128:(h + 1) * 128],
                    scalar=stat[:, so + 5 + h:so + 6 + h], in1=gainbc[:, h * 128:(h + 1) * 128],
                    op0=ALU.mult, op1=ALU.mult), [("vv", v2), ("yb", pb_), "gainbc"], [("vn", u3, h)])

        def st_D(q):
            u3 = q % 3
            UU, VN, AA = UU3[u3], VN3[u3], AA1
            lo, hi, i0, n = geom(q)
            bs_ = 5

            def mms(e, bs_=bs_, VN=VN):
                ins = None
                for h in range(4):
                    ins = e.matmul(PS[:, bs_ * 512 + h * 128:bs_ * 512 + (h + 1) * 128],
                                   lhsT=wsT[:, h * 128:(h + 1) * 128], rhs=VN[:, h * 128:(h + 1) * 128],
                                   start=True, stop=True)
                return ins
            S.op("pe", mms, [("vn", u3, h) for h in range(4)] + ["wsT"], [("ps", bs_)])
            for h in range(4):
                S.op("dve", lambda e, h=h, bs_=bs_, AA=AA, UU=UU: e.scalar_tensor_tensor(
                    out=AA[:, h * 128:(h + 1) * 128], in0=PS[:, bs_ * 512 + h * 128:bs_ * 512 + (h + 1) * 128],
                    scalar=bst[:, h:h + 1], in1=UU[:, h * 128:(h + 1) * 128], op0=ALU.add, op1=ALU.mult),
                    [("ps", bs_), ("uu", u3), "bst"], [("aa", h)])
        def st_Db(q):
            AA = AA1
            lo, hi, i0, n = geom(q)
            ba = 6
            pba = bank(ba).bitcast(BF16)

            def tra(e, pba=pba, AA=AA):
                ins = None
                for h in range(4):
                    ins = e.transpose(out=pba[:, h * 128:(h + 1) * 128], in_=AA[:, h * 128:(h + 1) * 128],
                                      identity=identb[:, :])
                return ins
            S.op("pe", tra, [("aa", h) for h in range(4)] + ["identb"], [("ps", ba)])
            S.op("act", lambda e, pba=pba, lo=lo, n=n, i0=i0: e.copy(
                out=BIG[:, 0:4, lo:lo + n],
                in_=pba[:, 0:512].rearrange("p (c t) -> p c t", c=4)[:, :, i0:i0 + n]),
                [("ps", ba)], blks("BIG", range(4), lo, hi))

        S.op("dve", lambda e: e.memset(stat[:, 16:48], 1.0), [], [("msb", 0), ("msb", 1), ("yb", 0), ("yb", 1)])
        for t in range(NQ + 5):
            if t < NQ:
                st_A1_dma(t)
            if 0 <= t - 5 < NQ:
                st_D(t - 5)
            if 0 <= t - 4 < NQ:
                st_C2(t - 4)
            if 0 <= t - 1 < NQ:
                st_A2(t - 1)
            if 0 <= t - 2 < NQ:
                st_B(t - 2)
            if t < NQ:
                st_A1(t)
            if 0 <= t - 3 < NQ:
                st_C1(t - 3)
            if 0 <= t - 5 < NQ:
                st_Db(t - 5)
            p = t % 2
            newton_rsqrt(sb(p), sb(p) + 5, sb(p) + 10, 5, ("msb", p), ("yb", p), ("tb", p))

    def dft_dense_p():
        MT = [SCR[:, 0:8192].rearrange("p (s k) -> p s k", s=16), SCR[:, 8192:16384].rearrange("p (s k) -> p s k", s=16)]
        PB = HR[:, 0:4096].rearrange("p (g r k) -> p g r k", g=4, r=2)
        ftoks = [("F", "p", q) for q in range(16)]
        it = 0
        for kt in range(4):
            for ri in range(2):
                ms = it % 2
                it += 1
                S.dma("sp", lambda e, ms=ms, kt=kt, ri=ri: e.dma_start(
                    out=SCR[:, ms * 8192:(ms + 1) * 8192], in_=MT_d[kt, ri]), [], [("mt", ms)], ("mt", ms))
                for g in range(4):
                    b = nb()

                    def mm(e, ms=ms, g=g, b=b):
                        ins = None
                        for s2 in range(16):
                            ins = e.matmul(bank(b), lhsT=FPR[:, s2, g * 128:(g + 1) * 128], rhs=MT[ms][:, s2, :],
                                           start=(s2 == 0), stop=(s2 == 15))
                        return ins
                    S.op("pe", mm, ftoks + [("mt", ms)], [("ps", b)])
                    if (g + ri) % 2 == 0:
                        S.op("act", lambda e, g=g, ri=ri, b=b: e.copy(out=PB[:, g, ri, :], in_=bank(b)),
                             [("ps", b)], [("pb", g, ri)])
                    else:
                        S.op("dve", lambda e, g=g, ri=ri, b=b: e.tensor_copy(out=PB[:, g, ri, :], in_=bank(b)),
                             [("ps", b)], [("pb", g, ri)])
            for g in range(4):
                b = nb()

                def mmc(e, g=g, b=b):
                    e.matmul(bank(b), lhsT=CDt[:, 0:128], rhs=PB[:, g, 0, :], start=True, stop=False)
                    return e.matmul(bank(b), lhsT=CDt[:, 128:256], rhs=PB[:, g, 1, :], start=False, stop=True)
                S.op("pe", mmc, [("pb", g, 0), ("pb", g, 1), "CD"], [("ps", b)])
                e0 = HALO + kt * 512
                S.op("act", lambda e, g=g, b=b, e0=e0: e.copy(out=BIG[:, 4 + g, e0:e0 + 512], in_=bank(b)),
                     [("ps", b)], blk("BIG", 4 + g, e0, e0 + 512))

    def dft(slab):
        if slab == "p":
            return dft_dense_p()
        n = 16 if slab == "p" else 128
        nk1 = 128 if slab == "p" else 18
        NB = 2 * nk1
        NK = nk1 * n
        S_len = n * 128
        MB1 = HR[:, 0:n * NB].rearrange("p (k j) -> p k j", k=n)
        MB2 = HR[:, 4608:4608 + n * NB].rearrange("p (k j) -> p k j", k=n)
        P = HR[:, 9216:9216 + 2 * NK].rearrange("p (r k) -> p r k", r=2)
        S.dma("sp", lambda e: e.dma_start(out=HR[:, 0:n * NB], in_=MB1_d[slab].rearrange("p k j -> p (k j)")),
              [], ["MB1"], ("c", "MB1"))
        S.dma("sp", lambda e: e.dma_start(out=HR[:, 4608:4608 + n * NB],
                                          in_=MB2_d[slab].rearrange("p k j -> p (k j)")),
              [], ["MB2"], ("c", "MB2"))
        S.dma("sp", lambda e: e.dma_start(out=WAt[0:n, 0:2 * n], in_=WA_d[slab][:, :]), [], ["WA"], ("c", "WA"))
        Y = XR[:, 0:16384].bitcast(BF16).rearrange("p (c k) -> p c k", c=128)
        XB = SCR[:, 0:16384].rearrange("p (s c) -> p s c", s=128)
        cpb = 512 // (2 * n)
        kpb = 512 // NB
        if slab == "p":
            e_lo, idx_lo, ncols = 3, 0, 2048
        else:
            e_lo, idx_lo, ncols = 0, 125, 2054
        for g in range(4):
            S.dma("sp", lambda e, g=g: e.dma_start(
                out=XB[0:n, :, :], in_=Fd[slab][g].rearrange("(s2 s1) c -> s2 s1 c", s1=128)),
                [("F", slab, q) for q in range(n)], [("xb",)], ("xb",))
            for c0 in range(0, 128, cpb):
                b = nb()

                def mma(e, c0=c0, b=b):
                    ins = None
                    for j in range(cpb):
                        ins = e.matmul(PS[:, b * 512 + j * 2 * n:b * 512 + (j + 1) * 2 * n],
                                       lhsT=XB[0:n, :, c0 + j], rhs=WAt[0:n, 0:2 * n], start=True, stop=True)
                    return ins
                S.op("pe", mma, [("xb",), "WA"], [("ps", b)])
                eng = "act" if (c0 // cpb) % 2 == 0 else "dve"
                if eng == "act":
                    S.op("act", lambda e, c0=c0, b=b: e.copy(
                        out=Y[:, c0:c0 + cpb, 0:2 * n],
                        in_=bank(b).rearrange("p (c k) -> p c k", c=cpb)), [("ps", b)], [("Y", c0)])
                else:
                    S.op("dve", lambda e, c0=c0, b=b: e.tensor_copy(
                        out=Y[:, c0:c0 + cpb, 0:2 * n],
                        in_=bank(b).rearrange("p (c k) -> p c k", c=cpb)), [("ps", b)], [("Y", c0)])
            ytoks = [("Y", c0) for c0 in range(0, 128, cpb)]
            for k0 in range(0, n, kpb):
                kn = min(kpb, n - k0)
                b = nb()

                def mmb(e, k0=k0, kn=kn, b=b):
                    ins = None
                    for j in range(kn):
                        k2 = k0 + j
                        o = PS[:, b * 512 + j * NB:b * 512 + (j + 1) * NB]
                        e.matmul(o, lhsT=Y[:, :, k2], rhs=MB1[:, k2, :], start=True, stop=False)
                        ins = e.matmul(o, lhsT=Y[:, :, n + k2], rhs=MB2[:, k2, :], start=False, stop=True)
                    return ins
                S.op("pe", mmb, ytoks + ["MB1", "MB2"], [("ps", b)])
                src = PS[:, b * 512:b * 512 + kn * NB].rearrange("p (k r i) -> p k r i", k=kn, r=2)
                dst = P.rearrange("p r (i k) -> p k r i", k=n)[:, k0:k0 + kn, :, :]
                if (k0 // kpb) % 2 == 0:
                    S.op("act", lambda e, s=src, d=dst: e.copy(out=d, in_=s), [("ps", b)], [("P", k0)])
                else:
                    S.op("dve", lambda e, s=src, d=dst: e.tensor_copy(out=d, in_=s), [("ps", b)], [("P", k0)])
            ptoks = [("P", k0) for k0 in range(0, n, kpb)]
            t0 = 0
            while t0 < ncols:
                tn = min(512, ncols - t0)
                b = nb()

                def mmc(e, t0=t0, tn=tn, b=b):
                    e.matmul(bank(b, tn), lhsT=CDt[:, 0:128], rhs=P[:, 0, idx_lo + t0:idx_lo + t0 + tn],
                             start=True, stop=False)
                    return e.matmul(bank(b, tn), lhsT=CDt[:, 128:256], rhs=P[:, 1, idx_lo + t0:idx_lo + t0 + tn],
                                    start=False, stop=True)
                S.op("pe", mmc, ptoks + ["CD"], [("ps", b)])
                S.op("act", lambda e, t0=t0, tn=tn, b=b, g=g: e.copy(
                    out=BIG[:, 4 + g, e_lo + t0:e_lo + t0 + tn], in_=bank(b, tn)),
                    [("ps", b)], blk("BIG", 4 + g, e_lo + t0, e_lo + t0 + tn))
                t0 += tn

    def mask_left():
        S.op("dve", lambda e: e.tensor_tensor(out=X[:, :, 0:3], in0=X[:, :, 0:3],
                                              in1=maskt[:, 0:48].rearrange("p (c m) -> p c m", c=8)[:, :, 0:3],
                                              op=ALU.mult),
             blks("X", range(8), 0, 3) + ["mask"], blks("X", range(8), 0, 3))

    def mask_right():
        S.op("dve", lambda e: e.tensor_tensor(out=X[:, :, W - 3:W], in0=X[:, :, W - 3:W],
                                              in1=maskt[:, 0:48].rearrange("p (c m) -> p c m", c=8)[:, :, 3:6],
                                              op=ALU.mult),
             blks("X", range(8), W - 3, W) + ["mask"], blks("X", range(8), W - 3, W))

    def mask_halo():
        mask_left()
        mask_right()

    SQ = SCR[:, 0:4096].rearrange("p (c n) -> p c n", c=8)
    SD = scr_f32(4096, 512)
    CG = [scr_f32(6144 + i_ * 1056, 528) for i_ in (0, 1)]
    CV = [scr_f32(6144 + i_ * 1056, 528) for i_ in (2, 3)]
    SG = [scr_f32(6144 + i_ * 1056, 528) for i_ in (4, 5)]
    YT = scr_f32(12288, 1024)
    OST = [scr_f32(14336, 1024)]

    RS2 = [scr_f32(5120, 512), scr_f32(16384, 512)]
    rstate = {"i": 0}

    def rms_tile(t0, tn):
        par = rstate["i"] % 2
        rstate["i"] += 1
        RSp = RS2[par]
        S.op("act", lambda e: e.activation(out=SQ[:, :, 0:tn], in_=X[:, :, t0:t0 + tn], func=AF.Square),
             blks("X", range(8), t0, t0 + tn), [("sq",)])
        b = nb()

        def mm(e, b=b):
            ins = None
            for c in range(8):
                ins = e.matmul(bank(b, tn), lhsT=onesb[:, :], rhs=SQ[:, c, 0:tn], start=(c == 0), stop=(c == 7))
            return ins
        S.op("pe", mm, [("sq",), "ones"], [("ps", b)])
        S.op("act", lambda e, b=b: e.activation(out=SD[:, 0:tn], in_=bank(b, tn), func=AF.Ln,
                                                 bias=epsT[:, 0:1], scale=float(1.0 / D)),
             [("ps", b), "eps"], [("sd",)])
        S.op("act", lambda e: e.activation(out=RSp[:, 0:tn], in_=SD[:, 0:tn], func=AF.Exp, scale=-0.5),
             [("sd",)], [("rs", par)])
        return RSp, ("rs", par)

    def norm_to_H(gi):
        S.op("pool", lambda e: e.memset(H[:, :, 0:1], 0.0), [], blks("H", range(8), 0, 1))
        S.op("pool", lambda e: e.memset(H[:, :, 2055:2056], 0.0), [], blks("H", range(8), 2055, 2056))
        for (t0, tn) in TT:
            RSp, rtok = rms_tile(t0, tn)
            for c in range(8):
                S.op("dve", lambda e, c=c, t0=t0, tn=tn, RSp=RSp: e.scalar_tensor_tensor(
                    out=H[:, c, 1 + t0:1 + t0 + tn], in0=X[:, c, t0:t0 + tn],
                    scalar=gvec[:, gi * 8 + c:gi * 8 + c + 1], in1=RSp[:, 0:tn], op0=ALU.mult, op1=ALU.mult),
                    blk("X", c, t0, t0 + tn) + [rtok, "gvec"], blk("H", c, 1 + t0, 1 + t0 + tn))

    def linear_to_X(wsrc_fn, nk, rhs_fn, rhs_toks_fn, tiles=None):
        for o in range(8):
            wb, tok = wtile([wsrc_fn(o)], nk * 128)
            for (t0, tn) in (tiles or TT):
                b = nb()

                def mm(e, wb=wb, t0=t0, tn=tn, b=b):
                    ins = None
                    for k in range(nk):
                        ins = e.matmul(bank(b, tn), lhsT=wb[:, k * 128:(k + 1) * 128], rhs=rhs_fn(k, t0, tn),
                                       start=(k == 0), stop=(k == nk - 1))
                    return ins
                S.op("pe", mm, [tok] + rhs_toks_fn(t0, tn), [("ps", b)])
                S.op("dve", lambda e, o=o, t0=t0, tn=tn, b=b: e.tensor_tensor(
                    out=X[:, o, t0:t0 + tn], in0=bank(b, tn), in1=X[:, o, t0:t0 + tn], op=ALU.add),
                    [("ps", b)] + blk("X", o, t0, t0 + tn), blk("X", o, t0, t0 + tn))

    def ffn(l):
        norm_to_H(1 + l)
        fw = fcw[:, l * 132:(l + 1) * 132].rearrange("p (a b) -> p a b", a=3)
        fb = fcb[:, l * 44:(l + 1) * 44]
        def up_pair(i):
            slot = i % 8
            if True:
                wb, tok = wtile(ffn_up_srcs(l, i), 2048)
                for ti, (h0, hn, nout) in enumerate(CT[:3] + [(1530, 526, 524)]):
                    merged = hn > 512
                    if merged:
                        bg, bv = nb2(), nb2()
                    else:
                        bg, bv = nb(), nb()

                    def mm(e, wb=wb, h0=h0, hn=hn, bg=bg, bv=bv, merged=merged):
                        ins = None
                        for (bb_, wo) in ((bg, 0), (bv, 1024)):
                            for c in range(8):
                                lt = wb[:, wo + c * 128:wo + (c + 1) * 128]
                                ins = e.matmul(bank(bb_, min(hn, 512)), lhsT=lt, rhs=H[:, c, h0:h0 + min(hn, 512)],
                                               start=(c == 0), stop=(c == 7))
                                if merged:
                                    ins = e.matmul(bank(bb_ + 1, hn - 512), lhsT=lt, rhs=H[:, c, h0 + 512:h0 + hn],
                                                   start=(c == 0), stop=(c == 7))
                        return ins
                    pall = (lambda bb_: [("ps", bb_), ("ps", bb_ + 1)]) if merged else (lambda bb_: [("ps", bb_)])
                    S.op("pe", mm, [tok] + blks("H", range(8), h0, h0 + hn), pall(bg) + pall(bv))
                    sl = uid() % 2
                    jg, jv = i, NFC + i
                    for (bb, dst, j, nm) in ((bg, CG[sl], jg, "cg"), (bv, CV[sl], jv, "cv")):
                        S.op("act", lambda e, bb=bb, dst=dst, j=j, nout=nout: e.activation(
                            out=dst[:, 0:nout], in_=PS[:, bb * 512 + 1:bb * 512 + 1 + nout], func=AF.Identity,
                            bias=fb[:, j:j + 1], scale=fw[:, 1, j:j + 1]),
                            pall(bb) + ["fcw", "fcb"], [(nm, sl)])
                        S.op("dve", lambda e, bb=bb, dst=dst, j=j, nout=nout: e.scalar_tensor_tensor(
                            out=dst[:, 0:nout], in0=PS[:, bb * 512:bb * 512 + nout], scalar=fw[:, 0, j:j + 1],
                            in1=dst[:, 0:nout], op0=ALU.mult, op1=ALU.add),
                            pall(bb) + [(nm, sl), "fcw"], [(nm, sl)])
                        S.op("dve", lambda e, bb=bb, dst=dst, j=j, nout=nout: e.scalar_tensor_tensor(
                            out=dst[:, 0:nout], in0=PS[:, bb * 512 + 2:bb * 512 + 2 + nout], scalar=fw[:, 2, j:j + 1],
                            in1=dst[:, 0:nout], op0=ALU.mult, op1=ALU.add),
                            pall(bb) + [(nm, sl), "fcw"], [(nm, sl)])
                    S.op("act", lambda e, sl=sl, nout=nout: e.activation(out=SG[sl][:, 0:nout], in_=CG[sl][:, 0:nout],
                                                                          func=AF.Silu),
                         [("cg", sl)], [("sg", sl)])
                    S.op("pool", lambda e, sl=sl, nout=nout, slot=slot, h0=h0: e.tensor_tensor(
                        out=BIG[:, slot, h0:h0 + nout], in0=SG[sl][:, 0:nout], in1=CV[sl][:, 0:nout], op=ALU.mult),
                        [("sg", sl), ("cv", sl)], blk("BIG", slot, h0, h0 + nout))
        def down(grp, tiles=None):
            nk = len(grp)
            k0 = grp[0]
            linear_to_X(
                lambda o, k0=k0, nk=nk: ffn_down_src(l, k0, nk, o),
                nk, lambda k, t0, tn, k0=k0: BIG[:, (k0 + k) % 8, t0:t0 + tn],
                lambda t0, tn, nk=nk, k0=k0: blks("BIG", [(k0 + k) % 8 for k in range(nk)], t0, t0 + tn),
                tiles=tiles)

        for i in GROUPS[0]:
            up_pair(i)
        for g in range(1, len(GROUPS)):
            for i in GROUPS[g][:2]:
                up_pair(i)
            down(GROUPS[g - 1])
            for i in GROUPS[g][2:]:
                up_pair(i)
        down(GROUPS[-1], TT[:2])
        mask_left()
        down(GROUPS[-1], TT[2:])
        mask_right()

    MM_, TMP, CC = CG, CV, SG

    def mixer_c():
        norm_to_H(0)
        cw = ccw[:, :].rearrange("p (a b) -> p a b", a=3)
        for j in range(8):
            wb, tok = wtile(mixc_srcs(j), 3072)
            for (h0, hn, nout) in CT:
                n_ = uid()
                bb_, bc, bz = (0, 1, 2)[n_ % 3], (3, 4)[n_ % 2], (5, 6, 7)[n_ % 3]

                def mm(e, wb=wb, h0=h0, hn=hn, nout=nout, bb_=bb_, bc=bc, bz=bz):
                    ins = None
                    for c in range(8):
                        e.matmul(bank(bc, hn), lhsT=wb[:, 1024 + c * 128:1024 + (c + 1) * 128],
                                 rhs=H[:, c, h0:h0 + hn], start=(c == 0), stop=(c == 7))
                    for c in range(8):
                        e.matmul(bank(bz, hn), lhsT=wb[:, 2048 + c * 128:2048 + (c + 1) * 128],
                                 rhs=H[:, c, h0:h0 + hn], start=(c == 0), stop=(c == 7))
                    for c in range(8):
                        ins = e.matmul(bank(bb_, nout), lhsT=wb[:, c * 128:(c + 1) * 128],
                                       rhs=H[:, c, h0 + 1:h0 + 1 + nout], start=(c == 0), stop=(c == 7))
                    return ins
                S.op("pe", mm, [tok] + blks("H", range(8), h0, h0 + hn), [("ps", bb_), ("ps", bc), ("ps", bz)])
                sl = n_ % 2
                S.op("act", lambda e, sl=sl, hn=hn, bc=bc: e.copy(out=TMP[sl][:, 0:hn], in_=bank(bc, hn)),
                     [("ps", bc)], [("cv", sl)])
                S.op("dve", lambda e, sl=sl, hn=hn, bz=bz: e.tensor_tensor(
                    out=MM_[sl][:, 0:hn], in0=bank(bz, hn), in1=TMP[sl][:, 0:hn], op=ALU.mult),
                    [("ps", bz), ("cv", sl)], [("cg", sl)])
                S.op("act", lambda e, sl=sl, nout=nout, j=j: e.activation(
                    out=CC[sl][:, 0:nout], in_=MM_[sl][:, 1:1 + nout], func=AF.Identity,
                    bias=ccb[:, j:j + 1], scale=cw[:, 1, j:j + 1]), [("cg", sl), "ccw", "ccb"], [("sg", sl)])
                S.op("dve", lambda e, sl=sl, nout=nout, j=j: e.scalar_tensor_tensor(
                    out=CC[sl][:, 0:nout], in0=MM_[sl][:, 0:nout], scalar=cw[:, 0, j:j + 1],
                    in1=CC[sl][:, 0:nout], op0=ALU.mult, op1=ALU.add), [("cg", sl), ("sg", sl), "ccw"], [("sg", sl)])
                S.op("dve", lambda e, sl=sl, nout=nout, j=j: e.scalar_tensor_tensor(
                    out=CC[sl][:, 0:nout], in0=MM_[sl][:, 2:2 + nout], scalar=cw[:, 2, j:j + 1],
                    in1=CC[sl][:, 0:nout], op0=ALU.mult, op1=ALU.add), [("cg", sl), ("sg", sl), "ccw"], [("sg", sl)])
                S.op("dve", lambda e, sl=sl, nout=nout, j=j, h0=h0, bb_=bb_: e.tensor_tensor(
                    out=BIG[:, j, h0:h0 + nout], in0=bank(bb_, nout), in1=CC[sl][:, 0:nout], op=ALU.mult),
                    [("ps", bb_), ("sg", sl)], blk("BIG", j, h0, h0 + nout))
        for tiles, mfn in ((TT[:2], mask_left), (TT[2:], mask_right)):
            linear_to_X(lambda o: w_out_c[:, o * 128:(o + 1) * 128].rearrange("(k p) o -> p k o", p=128),
                        8, lambda k, t0, tn: BIG[:, k, t0:t0 + tn],
                        lambda t0, tn: blks("BIG", range(8), t0, t0 + tn), tiles=tiles)
            mfn()

    def final_out(slab):
        S.barrier()
        wprefetch_extra()
        YTs = [YT, scr_f32(6144, 1024)]
        OSTs = [OST[0], scr_f32(8192, 1024)]
        rs_of = {}

        def stage1(i):
            ti, qq = divmod(i, 4)
            t0 = HALO + ti * 512
            if i == 0:
                rs_of[0] = rms_tile(t0, 512)
            if qq == 1 and ti + 1 < 4:
                rs_of[ti + 1] = rms_tile(t0 + 512, 512)
            RSp, rtok = rs_of[ti]
            c0 = t0 + qq * 128
            yt = YTs[i % 2]
            for c in range(8):
                S.op("dve", lambda e, c=c, c0=c0, qq=qq, RSp=RSp, yt=yt: e.scalar_tensor_tensor(
                    out=yt[:, c * 128:(c + 1) * 128], in0=X[:, c, c0:c0 + 128],
                    scalar=gvec[:, 24 + c:25 + c], in1=RSp[:, qq * 128:(qq + 1) * 128],
                    op0=ALU.mult, op1=ALU.mult),
                    blk("X", c, c0, c0 + 128) + [rtok, "gvec"], [("yt", i % 2, c)])

        def stage2(i):
            yt, ost = YTs[i % 2], OSTs[i % 2]
            b2 = nb2()

            def tr(e, b2=b2, yt=yt):
                ins = None
                for c in range(8):
                    ins = e.transpose(out=PS[:, b2 * 512 + c * 128:b2 * 512 + (c + 1) * 128],
                                      in_=yt[:, c * 128:(c + 1) * 128], identity=identf[:, :])
                return ins
            S.op("pe", tr, [("yt", i % 2, c) for c in range(8)] + ["identf"], [("ps", b2), ("ps", b2 + 1)])
            S.op("act", lambda e, b2=b2, ost=ost: e.copy(out=ost, in_=PS[:, b2 * 512:b2 * 512 + 1024]),
                 [("ps", b2), ("ps", b2 + 1)], [("ost", i % 2)])
            r0 = i * 128
            S.dma("pool", lambda e, r0=r0, ost=ost: e.dma_start(out=yout[slab][r0:r0 + 128, :], in_=ost),
                  [("ost", i % 2)], [("yout", slab, r0)], ("ost", i % 2))

        for i in range(17):
            if i < 16:
                stage1(i)
            if i >= 1:
                stage2(i - 1)
        S.barrier()

    for slab in ("s", "p"):
        S.barrier()
        S.dma("sp", lambda e, slab=slab: e.dma_start(out=maskt[:, :], in_=maskd[slab].rearrange("p c m -> p (c m)")),
              [], ["mask"], ("c", "mask"))
        f_pass(slab, 128 if slab == "s" else 16)
        S.barrier()
        dft(slab)
        S.barrier()
        front(slab)
        S.barrier()
        for tiles, mfn in ((TT[:2], mask_left), (TT[2:], mask_right)):
            linear_to_X(lambda o: w_out_ab[:, o * 128:(o + 1) * 128].rearrange("(k p) o -> p k o", p=128),
                        8, lambda k, t0, tn: BIG[:, k, t0:t0 + tn],
                        lambda t0, tn: blks("BIG", range(8), t0, t0 + tn), tiles=tiles)
            mfn()
        ffn(0)
        mixer_c()
        ffn(1)
        final_out(slab)
    S.emit(nc)
    return nc


def _tables():
    t = {}
    t["identf"] = np.eye(128, dtype=np.float32)
    t["identb"] = np.eye(128, dtype=np.float32).astype(NPBF)
    c = np.arange(128)
    ang = 2 * np.pi * np.outer(c, c) / 128.0
    cd = np.stack([np.cos(ang), np.sin(ang)], axis=1) / np.sqrt(128.0)
    t["CD"] = cd.astype(np.float32).astype(NPBF)
    for slab, n in (("p", 16), ("s", 128)):
        s2 = np.arange(n)
        a = 2 * np.pi * np.outer(s2, s2) / n
        t["WA_" + slab] = np.concatenate([np.cos(a), -np.sin(a)], axis=1).astype(np.float32).astype(NPBF)
    return t


def _mt_prompt():
    s1 = np.arange(128, dtype=np.float64)[:, None, None]
    s2 = np.arange(16, dtype=np.float64)[None, :, None]
    out = np.zeros((4, 2, 128, 16, 512), dtype=np.float32)
    sc = 1.0 / np.sqrt(float(S_P))
    for kt in range(4):
        k = (kt * 512 + np.arange(512, dtype=np.float64))[None, None, :]
        ang = 2 * np.pi * np.mod((s2 * 128 + s1) * k, S_P) / S_P
        out[kt, 0] = np.cos(ang) * sc
        out[kt, 1] = -np.sin(ang) * sc
    return out.reshape(4, 2, 128, 16 * 512).astype(NPBF)


def _mb(S_len, n, k1_list):
    s1 = np.arange(128, dtype=np.float64)[:, None, None]
    k2 = np.arange(n, dtype=np.float64)[None, :, None]
    k1 = np.asarray(k1_list, dtype=np.float64)[None, None, :]
    k = np.mod(n * k1 + k2, S_len)
    ang = 2 * np.pi * np.mod(s1 * k, S_len) / S_len
    sc = 1.0 / np.sqrt(S_len)
    mr = np.cos(ang) * sc
    mi = -np.sin(ang) * sc
    mb1 = np.concatenate([mr, mi], axis=2)
    mb2 = np.concatenate([-mi, mr], axis=2)
    return mb1.astype(np.float32).astype(NPBF), mb2.astype(np.float32).astype(NPBF)


_CACHE = {}


def kernel(x_prompt, x_sample, norm_mix, w_in_ab, sgu_gain, w_s, b_s, w_out_ab,
           w_in_c, conv_w_c, conv_b_c, w_out_c, norm_ffn, w_up, ffn_conv_w, ffn_conv_b, w_down, final_norm):
    f32 = np.float32
    A = lambda a: np.ascontiguousarray(np.asarray(a, dtype=f32))
    x_prompt, x_sample = A(x_prompt), A(x_sample)
    if "nc" not in _CACHE:
        _CACHE["nc"] = build_program()
        _CACHE["tab"] = _tables()
    nc = _CACHE["nc"]
    tab = _CACHE["tab"]
    shared = {
        "w_in_ab": A(w_in_ab)[0], "w_out_ab": A(w_out_ab)[0], "w_in_c": A(w_in_c)[0], "w_out_c": A(w_out_c)[0],
        "w_up": A(w_up), "w_down": A(w_down),
        "g0bc": np.ascontiguousarray(np.broadcast_to(A(norm_mix)[0][None, :], (128, D))),
        "gainbc": np.ascontiguousarray(np.broadcast_to(A(sgu_gain)[0].reshape(1, 512), (128, 512))),
        "bst": np.ascontiguousarray(A(b_s)[0].T),
        "wsT": np.ascontiguousarray(A(w_s)[0].transpose(2, 0, 1)),
        "gvec": np.ascontiguousarray(np.stack([A(norm_mix)[1], A(norm_ffn)[0], A(norm_ffn)[1], A(final_norm)], 0)
                                     .reshape(4, 8, 128).transpose(2, 0, 1)),
        "ccw": np.ascontiguousarray(A(conv_w_c)[0].reshape(3, 8, 128).transpose(2, 0, 1)),
        "ccb": np.ascontiguousarray(A(conv_b_c)[0].reshape(8, 128).T),
        "fcw": np.ascontiguousarray(A(ffn_conv_w).reshape(2, 3, 44, 128).transpose(3, 0, 1, 2)),
        "fcb": np.ascontiguousarray(A(ffn_conv_b).reshape(2, 44, 128).transpose(2, 0, 1)),
        "identf": tab["identf"], "identb": tab["identb"], "CD": tab["CD"],
        "WA_p": tab["WA_p"], "WA_s": tab["WA_s"],
        "xa_s": x_sample[0],
    }
    mb1p, mb2p = _mb(S_P, 16, list(range(128)))
    shared["MB1_p"], shared["MB2_p"] = mb1p, mb2p
    if "mt" not in _CACHE:
        _CACHE["mt"] = _mt_prompt()
    shared["MT_p"] = _CACHE["mt"]
    in_maps = []
    for j in range(NCORE):
        m = dict(shared)
        m["xa_p"] = x_prompt[j]
        xe_p = np.zeros((18 * 128, D), f32)
        xe_p[128:128 + S_P] = x_prompt[j]
        m["xe_p"] = xe_p
        xe_s = np.zeros((18 * 128, D), f32)
        lo = 2048 * j - 128
        hi = lo + 18 * 128
        slo, shi = max(lo, 0), min(hi, S_S)
        xe_s[slo - lo:shi - lo] = x_sample[0, slo:shi]
        m["xe_s"] = xe_s
        m["mask_p"] = np.zeros((128, 8, 6), f32)
        ms = np.ones((128, 8, 6), f32)
        if j == 0:
            ms[:, :, 0:3] = 0
        if j == NCORE - 1:
            ms[:, :, 3:6] = 0
        m["mask_s"] = ms
        mb1, mb2 = _mb(S_S, 128, [16 * j - 1 + i for i in range(18)])
        m["MB1_s"], m["MB2_s"] = mb1, mb2
        in_maps.append(m)
    res = run_bass_kernel_spmd(nc, in_maps, core_ids=list(range(NCORE)))
    yp = np.stack([np.asarray(res.results[j]["y_p"], dtype=f32) for j in range(NCORE)], 0)
    ys = np.concatenate([np.asarray(res.results[j]["y_s"], dtype=f32) for j in range(NCORE)], 0)[None]
    return yp, ys
```

```python
import numpy as np
import ml_dtypes
import concourse.bass as bass
import concourse.mybir as mybir
from concourse.bass_utils import run_bass_kernel_spmd

F32 = mybir.dt.float32
BF16 = mybir.dt.bfloat16
I32 = mybir.dt.int32
AF = mybir.ActivationFunctionType
ALU = mybir.AluOpType
NPBF = ml_dtypes.bfloat16

D = 1024
S_P = 2048
S_S = 16384
NCORE = 8
W = 2054
HALO = 3
E0 = 125
DFF = 2816
NFC = 22
EPS = 1e-6
TT = [(0, 512), (512, 512), (1024, 512), (1536, 512), (2048, 6)]
CT = [(510 * n, 512, 510) for n in range(4)] + [(2040, 16, 14)]
GROUPS = [list(range(0, 6)), list(range(6, 12)), list(range(12, 17)), list(range(17, 22))]


class Op:
    __slots__ = ("eng", "fn", "deps", "sig", "sigval", "kind", "key", "dval")


class Sched:
    ENGS = ["sp", "act", "dve", "pool", "pe"]

    def __init__(self):
        self.ops = {e: [] for e in self.ENGS}
        self.lastw = {}
        self.readers = {}
        self.bar = []
        self.keycnt = {}
        self.lastdma = {}

    def _mk(self, eng, fn, R, Wr):
        o = Op()
        o.eng, o.fn, o.sig, o.sigval, o.kind, o.key, o.dval = eng, fn, False, 0, "c", None, 0
        deps = set(self.bar)
        for r in R:
            w = self.lastw.get(r)
            if w is not None:
                deps.add(w)
        for r in Wr:
            w = self.lastw.get(r)
            if w is not None:
                deps.add(w)
            for rd in self.readers.get(r, ()):
                deps.add(rd)
        for r in R:
            self.readers.setdefault(r, []).append(o)
        for r in Wr:
            self.lastw[r] = o
            self.readers[r] = []
        deps.discard(o)
        o.deps = deps
        for d in deps:
            if d.kind == "c":
                d.sig = True
        self.ops[eng].append(o)
        return o

    def op(self, eng, fn, R=(), Wr=()):
        return self._mk(eng, fn, R, Wr)

    def dma(self, eng, fn, R, Wr, key):
        o = self._mk(eng, fn, R, Wr)
        o.kind = "d"
        o.key = key
        self.keycnt[key] = self.keycnt.get(key, 0) + 1
        o.dval = 16 * self.keycnt[key]
        self.lastdma[key] = o
        return o

    def barrier(self):
        b = []
        for e in self.ENGS:
            for o in reversed(self.ops[e]):
                if o.kind == "c":
                    o.sig = True
                    b.append(o)
                    break
        b.extend(self.lastdma.values())
        self.bar = b

    def emit(self, nc):
        sems = {e: nc.alloc_semaphore("s_" + e) for e in self.ENGS}
        dsem = {k: nc.alloc_semaphore("d_%d" % i) for i, k in enumerate(self.keycnt)}
        for e in self.ENGS:
            cnt = 0
            for o in self.ops[e]:
                if o.kind == "c" and o.sig:
                    cnt += 1
                    o.sigval = cnt
        fin = []
        for e in self.ENGS:
            for o in reversed(self.ops[e]):
                if o.kind == "c" and o.sig:
                    fin.append((sems[e], o.sigval))
                    break
        for k, o in self.lastdma.items():
            fin.append((dsem[k], o.dval))
        ops = self.ops

        def make(ename):
            def body(eng):
                waited = {}
                for o in ops[ename]:
                    need = {}
                    for d in o.deps:
                        if d.kind == "c":
                            if d.eng == "pe" and ename == "pe":
                                continue
                            sem, val = sems[d.eng], d.sigval
                        else:
                            sem, val = dsem[d.key], d.dval
                        if waited.get(id(sem), 0) >= val:
                            continue
                        if need.get(id(sem), (None, 0))[1] < val:
                            need[id(sem)] = (sem, val)
                    for sid, (sem, val) in need.items():
                        eng.wait_ge(sem, val)
                        waited[sid] = val
                    ins = o.fn(eng)
                    if o.kind == "c":
                        if o.sig:
                            ins.then_inc(sems[ename], 1)
                    else:
                        ins.then_inc(dsem[o.key], 16)
                if ename == "sp":
                    for sem, val in fin:
                        eng.wait_ge(sem, val)
            return body

        with nc.Block() as block:
            block.sync(make("sp"))
            block.scalar(make("act"))
            block.vector(make("dve"))
            block.gpsimd(make("pool"))
            block.tensor(make("pe"))


def blk(name, c, lo, hi):
    return [(name, c, b) for b in range(lo // 512, (hi - 1) // 512 + 1)]


def blks(name, cs, lo, hi):
    r = []
    for c in cs:
        r += blk(name, c, lo, hi)
    return r


def build_program():
    nc = bass.Bass("TRN2", target_bir_lowering=False)
    S = Sched()

    def din(name, shape, dt=F32):
        return nc.dram_tensor(name, list(shape), dt, kind="ExternalInput").ap()

    xa = {"p": din("xa_p", [S_P, D]), "s": din("xa_s", [S_S, D])}
    xe = {"p": din("xe_p", [18 * 128, D]), "s": din("xe_s", [18 * 128, D])}
    maskd = {"p": din("mask_p", [128, 8, 6]), "s": din("mask_s", [128, 8, 6])}
    w_in_ab = din("w_in_ab", [D, 1536])
    w_out_ab = din("w_out_ab", [D, D])
    w_in_c = din("w_in_c", [D, 3072])
    w_out_c = din("w_out_c", [D, D])
    w_up = din("w_up", [2, D, 2 * DFF])
    w_down = din("w_down", [2, DFF, D])
    g0bc_d = din("g0bc", [128, D])
    gainbc_d = din("gainbc", [128, 512])
    bst_d = din("bst", [128, 4])
    wsT_d = din("wsT", [128, 4, 128])
    gvec_d = din("gvec", [128, 4, 8])
    ccw_d = din("ccw", [128, 3, 8])
    ccb_d = din("ccb", [128, 8])
    fcw_d = din("fcw", [128, 2, 3, 44])
    fcb_d = din("fcb", [128, 2, 44])
    identf_d = din("identf", [128, 128])
    identb_d = din("identb", [128, 128], BF16)
    WA_d = {"p": din("WA_p", [16, 32], BF16), "s": din("WA_s", [128, 256], BF16)}
    MB1_d = {"p": din("MB1_p", [128, 16, 256], BF16), "s": din("MB1_s", [128, 128, 36], BF16)}
    MB2_d = {"p": din("MB2_p", [128, 16, 256], BF16), "s": din("MB2_s", [128, 128, 36], BF16)}
    CD_d = din("CD", [128, 2, 128], BF16)
    MT_d = din("MT_p", [4, 2, 128, 16 * 512], BF16)
    yout = {"p": nc.dram_tensor("y_p", [S_P, D], F32, kind="ExternalOutput").ap(),
            "s": nc.dram_tensor("y_s", [S_P, D], F32, kind="ExternalOutput").ap()}
    Fd = {"p": nc.dram_tensor("F_p", [4, S_P, 128], BF16).ap(),
          "s": nc.dram_tensor("F_s", [4, S_S, 128], BF16).ap()}

    XR = nc.alloc_sbuf_tensor("XR", [128, 8 * W], F32)
    HR = nc.alloc_sbuf_tensor("HR", [128, 8 * 2056], BF16)
    BR = nc.alloc_sbuf_tensor("BR", [128, 8 * W], BF16)
    WST = [nc.alloc_sbuf_tensor("WST%d" % i, [128, 3072], F32) for i in range(2)]
    WBF = [nc.alloc_sbuf_tensor("WBF%d" % i, [128, 3072], BF16) for i in range(2)]
    SCR = nc.alloc_sbuf_tensor("SCR", [128, 17408], BF16)
    PS = nc.alloc_psum_tensor("PS", [128, 8 * 512], F32)
    bst = nc.alloc_sbuf_tensor("bst_s", [128, 4], F32)
    wsT = nc.alloc_sbuf_tensor("wsT_s", [128, 512], BF16)
    gvec = nc.alloc_sbuf_tensor("gvec_s", [128, 32], F32)
    ccw = nc.alloc_sbuf_tensor("ccw_s", [128, 24], F32)
    ccb = nc.alloc_sbuf_tensor("ccb_s", [128, 8], F32)
    fcw = nc.alloc_sbuf_tensor("fcw_s", [128, 264], F32)
    fcb = nc.alloc_sbuf_tensor("fcb_s", [128, 88], F32)
    identf = nc.alloc_sbuf_tensor("identf_s", [128, 128], F32)
    identb = nc.alloc_sbuf_tensor("identb_s", [128, 128], BF16)
    onesb = nc.alloc_sbuf_tensor("onesb", [128, 128], BF16)
    epsT = nc.alloc_sbuf_tensor("epsT", [128, 1], F32)
    maskt = nc.alloc_sbuf_tensor("maskt", [128, 48], F32)
    CDt = nc.alloc_sbuf_tensor("CDt", [128, 256], BF16)
    WAt = nc.alloc_sbuf_tensor("WAt", [128, 256], BF16)
    stat = nc.alloc_sbuf_tensor("stat", [128, 64], F32)

    X = XR[:, :].rearrange("p (c n) -> p c n", c=8)
    H = HR[:, :].rearrange("p (c n) -> p c n", c=8)
    BIG = BR[:, :].rearrange("p (c n) -> p c n", c=8)

    def bank(b, n=512):
        return PS[:, b * 512:b * 512 + n]

    bstate = {"b": 0}

    def nb():
        b = bstate["b"]
        bstate["b"] = (b + 1) % 8
        return b

    def nb2():
        if bstate["b"] % 2:
            bstate["b"] = (bstate["b"] + 1) % 8
        b = bstate["b"]
        bstate["b"] = (b + 2) % 8
        return b

    cnt = {"k": 0}

    def uid():
        cnt["k"] += 1
        return cnt["k"]

    def ld(dst, src, name):
        S.dma("sp", lambda e, d=dst, s=src: e.dma_start(out=d, in_=s), [], [name], ("c", name))

    ld(bst[:, :], bst_d[:, :], "bst")
    ld(WST[0][:, 0:512], wsT_d.rearrange("q h p -> q (h p)"), "wsTf")
    ld(gvec[:, :], gvec_d.rearrange("p a b -> p (a b)"), "gvec")
    ld(ccw[:, :], ccw_d.rearrange("p a b -> p (a b)"), "ccw")
    ld(ccb[:, :], ccb_d[:, :], "ccb")
    ld(fcw[:, :], fcw_d.rearrange("p l a b -> p (l a b)"), "fcw")
    ld(fcb[:, :], fcb_d.rearrange("p l b -> p (l b)"), "fcb")
    ld(identf[:, :], identf_d[:, :], "identf")
    ld(identb[:, :], identb_d[:, :], "identb")
    ld(CDt[:, :], CD_d.rearrange("p a b -> p (a b)"), "CD")
    S.op("pool", lambda e: e.memset(onesb[:, :], 1.0), [], ["ones"])
    S.op("pool", lambda e: e.memset(epsT[:, :], EPS), [], ["eps"])
    S.op("pool", lambda e: e.memset(BR[:, :], 0.0), [], ["BRz"])
    S.op("pool", lambda e: e.memset(HR[:, :], 0.0), [], ["HRz"])
    S.op("dve", lambda e: e.tensor_copy(out=wsT[:, :], in_=WST[0][:, 0:512]), ["wsTf"], ["wsT"])
    CONSTS = ["g0bc", "gainbc", "bst", "wsT", "gvec", "ccw", "ccb", "fcw", "fcb", "identf", "identb", "CD",
              "ones", "eps", "BRz", "HRz"]

    wplan = []
    wstate = {"i": 0, "issued": 0}

    def _issue(k):
        pieces, ncols = wplan[k]
        s = k % 2
        off = 0
        for pi, src in enumerate(pieces):
            a, b = src.shape[1], src.shape[2]
            dst = WST[s][:, off:off + a * b].rearrange("p (a b) -> p a b", a=a)
            S.dma("sp", lambda e, d=dst, sr=src: e.dma_start(out=d, in_=sr), [], [("wst", s, pi)], ("wst", s, pi))
            off += a * b
        assert off == ncols and ncols <= 3072
        S.op("act", lambda e, s=s, n=ncols: e.copy(out=WBF[s][:, 0:n], in_=WST[s][:, 0:n]),
             [("wst", s, pi) for pi in range(3)], [("wbf", s)] + [("wst", s, pi) for pi in range(3)])

    def wtile(pieces, ncols):
        k = wstate["i"]
        wstate["i"] += 1
        assert wplan[k][1] == ncols and len(wplan[k][0]) == len(pieces), (k, wplan[k][1], ncols)
        while wstate["issued"] <= min(k + 1, len(wplan) - 1):
            _issue(wstate["issued"])
            wstate["issued"] += 1
        return WBF[k % 2], ("wbf", k % 2)

    def wprefetch_extra():
        if wstate["issued"] < len(wplan):
            _issue(wstate["issued"])
            wstate["issued"] += 1

    def win_srcs(lo, n):
        return [([w_in_ab[:, lo + p0:lo + p0 + min(384, n - p0)].rearrange("(c p) o -> p c o", p=128)],
                 8 * min(384, n - p0)) for p0 in range(0, n, 384)]

    def wout_ab_src(o):
        return w_out_ab[:, o * 128:(o + 1) * 128].rearrange("(k p) o -> p k o", p=128)

    def wout_c_src(o):
        return w_out_c[:, o * 128:(o + 1) * 128].rearrange("(k p) o -> p k o", p=128)

    def ffn_up_srcs(l, i):
        return [w_up[l][:, i * 128:(i + 1) * 128].rearrange("(c p) o -> p c o", p=128),
                w_up[l][:, DFF + i * 128:DFF + (i + 1) * 128].rearrange("(c p) o -> p c o", p=128)]

    def ffn_down_src(l, k0, nk, o):
        return w_down[l][k0 * 128:(k0 + nk) * 128, o * 128:(o + 1) * 128].rearrange("(k p) o -> p k o", p=128)

    def mixc_srcs(j):
        return [w_in_c[:, kk * 1024 + j * 128:kk * 1024 + (j + 1) * 128].rearrange("(c p) o -> p c o", p=128)
                for kk in range(3)]

    def plan_ffn(l):
        up = lambda i: (ffn_up_srcs(l, i), 2048)
        dn = lambda grp: [([ffn_down_src(l, grp[0], len(grp), o)], len(grp) * 128) for o in range(8)]
        r = [up(i) for i in GROUPS[0]]
        for g in range(1, len(GROUPS)):
            r += [up(i) for i in GROUPS[g][:2]]
            r += dn(GROUPS[g - 1])
            r += [up(i) for i in GROUPS[g][2:]]
        r += dn(GROUPS[-1])
        r += dn(GROUPS[-1])
        return r

    for _slab in ("s", "p"):
        wplan += win_srcs(1024, 512)
        wplan += win_srcs(0, 1024)
        wplan += [([wout_ab_src(o)], 1024) for o in range(8)] * 2
        wplan += plan_ffn(0)
        wplan += [(mixc_srcs(j), 3072) for j in range(8)]
        wplan += [([wout_c_src(o)], 1024) for o in range(8)] * 2
        wplan += plan_ffn(1)

    def scr_f32(off, n):
        return SCR[:, off:off + 2 * n].bitcast(F32)

    FP_XT = [scr_f32(s_ * 4608, 1024) for s_ in range(3)]
    FP_XN = [SCR[:, s_ * 4608 + 2048:s_ * 4608 + 3072] for s_ in range(3)]
    FP_HT = [SCR[:, s_ * 4608 + 3072:s_ * 4608 + 4096] for s_ in range(3)]
    FB = [SCR[:, s_ * 4608 + 4096:s_ * 4608 + 4608] for s_ in range(3)]
    FR_XT = [scr_f32(s_ * 2048, 1024) for s_ in range(3)]
    FR_XN = [SCR[:, 6144 + s_ * 1024:6144 + (s_ + 1) * 1024] for s_ in range(3)]
    FR_HT = [SCR[:, 9216 + s_ * 1024:9216 + (s_ + 1) * 1024] for s_ in range(2)]
    VV2 = [scr_f32(11264 + s_ * 1024, 512) for s_ in range(2)]
    VN3 = [SCR[:, 13312:13824], SCR[:, 13824:14336], HR[:, 15360:15872]]
    UU3 = [HR[:, 12288 + s_ * 1024:12288 + (s_ + 1) * 1024].bitcast(F32) for s_ in range(3)]
    AA1 = HR[:, 15872:16384]
    g0bc = scr_f32(14336, 1024)
    gainbc = scr_f32(16384, 512)

    def load_gconsts():
        ld(g0bc, g0bc_d[:, :], "g0bc")
        ld(gainbc, gainbc_d[:, :], "gainbc")
    WIN = HR[:, 0:8 * 1536].rearrange("p (c n) -> p c n", c=8)

    def load_win(cols_lo, cols_n):
        toks = []
        for (pieces, ncols), p0 in zip(win_srcs(cols_lo, cols_n), range(0, cols_n, 384)):
            pn = ncols // 8
            wb, tok = wtile(pieces, ncols)
            S.op("act", lambda e, wb=wb, p0=p0, pn=pn: e.copy(
                out=WIN[:, :, p0:p0 + pn], in_=wb[:, 0:8 * pn].rearrange("p (c n) -> p c n", c=8)),
                [tok], [("win", p0)])
            toks.append(("win", p0))
        return toks

    def newton_rsqrt(a, b, c, n, tin, ty, tt_):
        xs, ys, ts = stat[:, a:a + n], stat[:, b:b + n], stat[:, c:c + n]
        xi, yi = xs.bitcast(I32), ys.bitcast(I32)
        S.op("dve", lambda e: e.tensor_scalar(out=xs, in0=xs, scalar1=EPS, scalar2=None, op0=ALU.add), [tin], [tin])
        S.op("dve", lambda e: e.tensor_scalar(out=yi, in0=xi, scalar1=1, scalar2=None, op0=ALU.arith_shift_right),
             [tin], [ty])
        S.op("dve", lambda e: e.tensor_scalar(out=yi, in0=yi, scalar1=-1, scalar2=0x5f3759df, op0=ALU.mult,
                                              op1=ALU.add), [ty], [ty])
        for _ in range(2):
            S.op("dve", lambda e: e.tensor_tensor(out=ts, in0=ys, in1=ys, op=ALU.mult), [ty], [tt_])
            S.op("dve", lambda e: e.scalar_tensor_tensor(out=ts, in0=ts, scalar=-0.5, in1=xs, op0=ALU.mult,
                                                         op1=ALU.mult), [tt_, tin], [tt_])
            S.op("dve", lambda e: e.scalar_tensor_tensor(out=ys, in0=ts, scalar=1.5, in1=ys, op0=ALU.add,
                                                         op1=ALU.mult), [ty, tt_], [ty])

    def chunk_A(src_rows, q, XT, XN, newton=False):
        sl = q % len(XT)
        S.dma("sp", lambda e, sl=sl, sr=src_rows: e.dma_start(out=XT[sl], in_=sr), [], [("xt", sl)], ("xt", sl))
        S.op("act", lambda e, sl=sl: e.activation(out=XN[sl], in_=XT[sl], func=AF.Square, scale=float(D ** -0.5),
                                                   accum_out=stat[:, 3 * sl:3 * sl + 1]),
             [("xt", sl)], [("xn", sl), ("ms", sl)])
        if newton:
            newton_rsqrt(3 * sl, 3 * sl + 2, 3 * sl + 1, 1, ("ms", sl), ("rs", sl), ("sd", sl))
        else:
            S.op("act", lambda e, sl=sl: e.activation(out=stat[:, 3 * sl + 1:3 * sl + 2],
                                                       in_=stat[:, 3 * sl:3 * sl + 1],
                                                       func=AF.Sqrt, bias=epsT[:, 0:1], scale=1.0),
                 [("ms", sl), "eps"], [("sd", sl)])
            S.op("dve", lambda e, sl=sl: e.reciprocal(out=stat[:, 3 * sl + 2:3 * sl + 3],
                                                      in_=stat[:, 3 * sl + 1:3 * sl + 2]),
                 [("sd", sl)], [("rs", sl)])
        S.op("dve", lambda e, sl=sl: e.scalar_tensor_tensor(out=XN[sl], in0=XT[sl],
                                                            scalar=stat[:, 3 * sl + 2:3 * sl + 3],
                                                            in1=g0bc, op0=ALU.mult, op1=ALU.mult),
             [("xt", sl), ("rs", sl), "g0bc"], [("xn", sl)])

    def chunk_B(q, XN, HT, b=None):
        sl = q % len(XN)
        if b is None:
            b = nb()
        pb = bank(b).bitcast(BF16)

        def tr(e, sl=sl, pb=pb):
            ins = None
            for c in range(8):
                ins = e.transpose(out=pb[:, c * 128:(c + 1) * 128], in_=XN[sl][:, c * 128:(c + 1) * 128],
                                  identity=identb[:, :])
            return ins
        S.op("pe", tr, [("xn", sl), "identb"], [("ps", b)])
        S.op("act", lambda e, sl=sl, pb=pb: e.copy(out=HT[sl], in_=pb[:, 0:1024]), [("ps", b)], [("ht", sl)])

    FPR = XR[:, 0:4096].bitcast(BF16).rearrange("p (s c) -> p s c", s=16)

    def f_pass(slab, nchunks):
        load_gconsts()
        wt = load_win(1024, 512)
        wprefetch_extra()

        def stage_C(q):
            sl = q % 3
            b = nb()

            def mm(e, sl=sl, b=b):
                ins = None
                for c in range(8):
                    ins = e.matmul(bank(b), lhsT=FP_HT[sl][:, c * 128:(c + 1) * 128], rhs=WIN[:, c, 0:512],
                                   start=(c == 0), stop=(c == 7))
                return ins
            S.op("pe", mm, [("ht", sl)] + wt, [("ps", b)])
            if slab == "p":
                S.op("dve", lambda e, q=q, b=b: e.tensor_copy(out=FPR[:, q, :], in_=bank(b)),
                     [("ps", b)], [("F", slab, q)])
                return
            S.op("dve", lambda e, sl=sl, b=b: e.tensor_copy(out=FB[sl], in_=bank(b)), [("ps", b)], [("fb", sl)])
            S.dma("pool", lambda e, sl=sl, q=q: e.dma_start(
                out=Fd[slab][:, q * 128:(q + 1) * 128, :].rearrange("g t c -> t g c"),
                in_=FB[sl].rearrange("p (g c) -> p g c", g=4)),
                [("fb", sl)], [("F", slab, q)], ("fst", sl))

        for t in range(nchunks + 2):
            if t < nchunks:
                chunk_A(xa[slab][t * 128:(t + 1) * 128, :], t, FP_XT, FP_XN)
            if 0 <= t - 1 < nchunks:
                chunk_B(t - 1, FP_XN, FP_HT)
            if 0 <= t - 2 < nchunks:
                stage_C(t - 2)

    def front(slab):
        load_gconsts()
        wt = load_win(0, 1024)
        NQ = 18

        def geom(q):
            lo = max(0, q * 128 - E0)
            hi = min(W, (q + 1) * 128 - E0)
            return lo, hi, lo + E0 - q * 128, hi - lo

        def sb(p):
            return 16 + p * 16

        def st_A1_dma(q):
            sl = q % 3
            S.dma("sp", lambda e, sl=sl, q=q: e.dma_start(out=FR_XT[sl], in_=xe[slab][q * 128:(q + 1) * 128, :]),
                  [], [("xt", sl)], ("xt", sl))

        def st_A1(q):
            sl = q % 3
            S.op("act", lambda e, sl=sl: e.activation(out=FR_XN[sl], in_=FR_XT[sl], func=AF.Square,
                                                       scale=float(D ** -0.5),
                                                       accum_out=stat[:, sb(q % 2) + 4:sb(q % 2) + 5]),
                 [("xt", sl)], [("xn", sl), ("msb", q % 2)])

        def st_A2(q):
            sl = q % 3
            yc = sb(q % 2) + 9
            S.op("dve", lambda e, sl=sl, yc=yc: e.scalar_tensor_tensor(out=FR_XN[sl], in0=FR_XT[sl],
                                                                       scalar=stat[:, yc:yc + 1],
                                                                       in1=g0bc, op0=ALU.mult, op1=ALU.mult),
                 [("xt", sl), ("yb", q % 2), "g0bc"], [("xn", sl)])

        def st_B(q):
            sl = q % 3
            hs = q % 2
            lo, hi, i0, n = geom(q)
            b = 0 if q % 2 == 0 else 7
            pb = bank(b).bitcast(BF16)

            def tr(e, sl=sl, pb=pb):
                ins = None
                for c in range(8):
                    ins = e.transpose(out=pb[:, c * 128:(c + 1) * 128], in_=FR_XN[sl][:, c * 128:(c + 1) * 128],
                                      identity=identb[:, :])
                return ins
            S.op("pe", tr, [("xn", sl), "identb"], [("ps", b)])
            S.op("act", lambda e, hs=hs, pb=pb: e.copy(out=FR_HT[hs], in_=pb[:, 0:1024]), [("ps", b)], [("ht", hs)])
            b2 = 1

            def trx(e, sl=sl, b2=b2):
                ins = None
                for c in range(8):
                    ins = e.transpose(out=PS[:, b2 * 512 + c * 128:b2 * 512 + (c + 1) * 128],
                                      in_=FR_XT[sl][:, c * 128:(c + 1) * 128], identity=identf[:, :])
                return ins
            S.op("pe", trx, [("xt", sl), "identf"], [("ps", b2), ("ps", b2 + 1)])
            S.op("act", lambda e, b2=b2, lo=lo, n=n, i0=i0: e.copy(
                out=X[:, :, lo:lo + n],
                in_=PS[:, b2 * 512:b2 * 512 + 1024].rearrange("p (c t) -> p c t", c=8)[:, :, i0:i0 + n]),
                [("ps", b2), ("ps", b2 + 1)], blks("X", range(8), lo, hi))

        def st_C1(q):
            hs, u3, v2 = q % 2, q % 3, q % 2
            UU, VV, VN = UU3[u3], VV2[v2], VN3[u3]
            so = sb((q + 3) % 2)
            bu, bv = 3, 4

            def mmu(e, hs=hs, bu=bu):
                ins = None
                for c in range(8):
                    ins = e.matmul(bank(bu), lhsT=FR_HT[hs][:, c * 128:(c + 1) * 128], rhs=WIN[:, c, 0:512],
                                   start=(c == 0), stop=(c == 7))
                return ins

            def mmv(e, hs=hs, bv=bv):
                ins = None
                for c in range(8):
                    ins = e.matmul(bank(bv), lhsT=FR_HT[hs][:, c * 128:(c + 1) * 128], rhs=WIN[:, c, 512:1024],
                                   start=(c == 0), stop=(c == 7))
                return ins
            S.op("pe", mmu, [("ht", hs)] + wt, [("ps", bu)])
            S.op("pe", mmv, [("ht", hs)] + wt, [("ps", bv)])
            S.op("act", lambda e, bu=bu, UU=UU: e.activation(out=UU, in_=bank(bu), func=AF.Gelu_apprx_tanh),
                 [("ps", bu)], [("uu", u3)])
            S.op("act", lambda e, bv=bv, VV=VV: e.activation(out=VV, in_=bank(bv), func=AF.Gelu_apprx_tanh),
                 [("ps", bv)], [("vv", v2)])
            for h in range(4):
                S.op("act", lambda e, h=h, VV=VV, VN=VN, so=so: e.activation(
                    out=VN[:, h * 128:(h + 1) * 128], in_=VV[:, h * 128:(h + 1) * 128],
                    func=AF.Square, scale=float(128 ** -0.5), accum_out=stat[:, so + h:so + h + 1]),
                    [("vv", v2)], [("vn", u3, h), ("msb", (q + 3) % 2)])

        def st_C2(q):
            u3, v2 = q % 3, q % 2
            VV, VN = VV2[v2], VN3[u3]
            pb_ = (q + 3) % 2
            so = sb(pb_)
            for h in range(4):
                S.op("dve", lambda e, h=h, VV=VV, VN=VN, so=so: e.scalar_tensor_tensor(
                    out=VN[:, h * 128:(h + 1) * 128], in0=VV[:, h * 128:(h + 1) * 128],
                    scalar=stat[:, so + 5 + h:so + 6 + h], in1=gainbc[:, h * 128:(h + 1) * 128],
                    op0=ALU.mult, op1=ALU.mult), [("vv", v2), ("yb", pb_), "gainbc"], [("vn", u3, h)])

        def st_D(q):
            u3 = q % 3
            UU, VN, AA = UU3[u3], VN3[u3], AA1
            lo, hi, i0, n = geom(q)
            bs_ = 5

            def mms(e, bs_=bs_, VN=VN):
                ins = None
                for h in range(4):
                    ins = e.matmul(PS[:, bs_ * 512 + h * 128:bs_ * 512 + (h + 1) * 128],
                                   lhsT=wsT[:, h * 128:(h + 1) * 128], rhs=VN[:, h * 128:(h + 1) * 128],
                                   start=True, stop=True)
                return ins
            S.op("pe", mms, [("vn", u3, h) for h in range(4)] + ["wsT"], [("ps", bs_)])
            for h in range(4):
                S.op("dve", lambda e, h=h, bs_=bs_, AA=AA, UU=UU: e.scalar_tensor_tensor(
                    out=AA[:, h * 128:(h + 1) * 128], in0=PS[:, bs_ * 512 + h * 128:bs_ * 512 + (h + 1) * 128],
                    scalar=bst[:, h:h + 1], in1=UU[:, h * 128:(h + 1) * 128], op0=ALU.add, op1=ALU.mult),
                    [("ps", bs_), ("uu", u3), "bst"], [("aa", h)])
        def st_Db(q):
            AA = AA1
            lo, hi, i0, n = geom(q)
            ba = 6
            pba = bank(ba).bitcast(BF16)

            def tra(e, pba=pba, AA=AA):
                ins = None
                for h in range(4):
                    ins = e.transpose(out=pba[:, h * 128:(h + 1) * 128], in_=AA[:, h * 128:(h + 1) * 128],
                                      identity=identb[:, :])
                return ins
            S.op("pe", tra, [("aa", h) for h in range(4)] + ["identb"], [("ps", ba)])
            S.op("act", lambda e, pba=pba, lo=lo, n=n, i0=i0: e.copy(
                out=BIG[:, 0:4, lo:lo + n],
                in_=pba[:, 0:512].rearrange("p (c t) -> p c t", c=4)[:, :, i0:i0 + n]),
                [("ps", ba)], blks("BIG", range(4), lo, hi))

        S.op("dve", lambda e: e.memset(stat[:, 16:48], 1.0), [], [("msb", 0), ("msb", 1), ("yb", 0), ("yb", 1)])
        for t in range(NQ + 5):
            if t < NQ:
                st_A1_dma(t)
            if 0 <= t - 5 < NQ:
                st_D(t - 5)
            if 0 <= t - 4 < NQ:
                st_C2(t - 4)
            if 0 <= t - 1 < NQ:
                st_A2(t - 1)
            if 0 <= t - 2 < NQ:
                st_B(t - 2)
            if t < NQ:
                st_A1(t)
            if 0 <= t - 3 < NQ:
                st_C1(t - 3)
            if 0 <= t - 5 < NQ:
                st_Db(t - 5)
            p = t % 2
            newton_rsqrt(sb(p), sb(p) + 5, sb(p) + 10, 5, ("msb", p), ("yb", p), ("tb", p))

    def dft_dense_p():
        MT = [SCR[:, 0:8192].rearrange("p (s k) -> p s k", s=16), SCR[:, 8192:16384].rearrange("p (s k) -> p s k", s=16)]
        PB = HR[:, 0:4096].rearrange("p (g r k) -> p g r k", g=4, r=2)
        ftoks = [("F", "p", q) for q in range(16)]
        it = 0
        for kt in range(4):
            for ri in range(2):
                ms = it % 2
                it += 1
                S.dma("sp", lambda e, ms=ms, kt=kt, ri=ri: e.dma_start(
                    out=SCR[:, ms * 8192:(ms + 1) * 8192], in_=MT_d[kt, ri]), [], [("mt", ms)], ("mt", ms))
                for g in range(4):
                    b = nb()

                    def mm(e, ms=ms, g=g, b=b):
                        ins = None
                        for s2 in range(16):
                            ins = e.matmul(bank(b), lhsT=FPR[:, s2, g * 128:(g + 1) * 128], rhs=MT[ms][:, s2, :],
                                           start=(s2 == 0), stop=(s2 == 15))
                        return ins
                    S.op("pe", mm, ftoks + [("mt", ms)], [("ps", b)])
                    if (g + ri) % 2 == 0:
                        S.op("act", lambda e, g=g, ri=ri, b=b: e.copy(out=PB[:, g, ri, :], in_=bank(b)),
                             [("ps", b)], [("pb", g, ri)])
                    else:
                        S.op("dve", lambda e, g=g, ri=ri, b=b: e.tensor_copy(out=PB[:, g, ri, :], in_=bank(b)),
                             [("ps", b)], [("pb", g, ri)])
            for g in range(4):
                b = nb()

                def mmc(e, g=g, b=b):
                    e.matmul(bank(b), lhsT=CDt[:, 0:128], rhs=PB[:, g, 0, :], start=True, stop=False)
                    return e.matmul(bank(b), lhsT=CDt[:, 128:256], rhs=PB[:, g, 1, :], start=False, stop=True)
                S.op("pe", mmc, [("pb", g, 0), ("pb", g, 1), "CD"], [("ps", b)])
                e0 = HALO + kt * 512
                S.op("act", lambda e, g=g, b=b, e0=e0: e.copy(out=BIG[:, 4 + g, e0:e0 + 512], in_=bank(b)),
                     [("ps", b)], blk("BIG", 4 + g, e0, e0 + 512))

    def dft(slab):
        if slab == "p":
            return dft_dense_p()
        n = 16 if slab == "p" else 128
        nk1 = 128 if slab == "p" else 18
        NB = 2 * nk1
        NK = nk1 * n
        S_len = n * 128
        MB1 = HR[:, 0:n * NB].rearrange("p (k j) -> p k j", k=n)
        MB2 = HR[:, 4608:4608 + n * NB].rearrange("p (k j) -> p k j", k=n)
        P = HR[:, 9216:9216 + 2 * NK].rearrange("p (r k) -> p r k", r=2)
        S.dma("sp", lambda e: e.dma_start(out=HR[:, 0:n * NB], in_=MB1_d[slab].rearrange("p k j -> p (k j)")),
              [], ["MB1"], ("c", "MB1"))
        S.dma("sp", lambda e: e.dma_start(out=HR[:, 4608:4608 + n * NB],
                                          in_=MB2_d[slab].rearrange("p k j -> p (k j)")),
              [], ["MB2"], ("c", "MB2"))
        S.dma("sp", lambda e: e.dma_start(out=WAt[0:n, 0:2 * n], in_=WA_d[slab][:, :]), [], ["WA"], ("c", "WA"))
        Y = XR[:, 0:16384].bitcast(BF16).rearrange("p (c k) -> p c k", c=128)
        XB = SCR[:, 0:16384].rearrange("p (s c) -> p s c", s=128)
        cpb = 512 // (2 * n)
        kpb = 512 // NB
        if slab == "p":
            e_lo, idx_lo, ncols = 3, 0, 2048
        else:
            e_lo, idx_lo, ncols = 0, 125, 2054
        for g in range(4):
            S.dma("sp", lambda e, g=g: e.dma_start(
                out=XB[0:n, :, :], in_=Fd[slab][g].rearrange("(s2 s1) c -> s2 s1 c", s1=128)),
                [("F", slab, q) for q in range(n)], [("xb",)], ("xb",))
            for c0 in range(0, 128, cpb):
                b = nb()

                def mma(e, c0=c0, b=b):
                    ins = None
                    for j in range(cpb):
                        ins = e.matmul(PS[:, b * 512 + j * 2 * n:b * 512 + (j + 1) * 2 * n],
                                       lhsT=XB[0:n, :, c0 + j], rhs=WAt[0:n, 0:2 * n], start=True, stop=True)
                    return ins
                S.op("pe", mma, [("xb",), "WA"], [("ps", b)])
                nh = n // 2 + 1
                ysrc = bank(b).rearrange("p (c r k) -> p c r k", c=cpb, r=2)[:, :, :, 0:nh]
                ydst = Y[:, c0:c0 + cpb, :].rearrange("p c (r k) -> p c r k", r=2)[:, :, :, 0:nh]
                eng = "act" if (c0 // cpb) % 2 == 0 else "dve"
                if eng == "act":
                    S.op("act", lambda e, s_=ysrc, d=ydst: e.copy(out=d, in_=s_), [("ps", b)], [("Y", c0)])
                else:
                    S.op("dve", lambda e, s_=ysrc, d=ydst: e.tensor_copy(out=d, in_=s_), [("ps", b)], [("Y", c0)])
            ytoks = [("Y", c0) for c0 in range(0, 128, cpb)]
            for k0 in range(0, n, kpb):
                kn = min(kpb, n - k0)
                b = nb()

                def mmb(e, k0=k0, kn=kn, b=b):
                    ins = None
                    for j in range(kn):
                        k2 = k0 + j
                        o = PS[:, b * 512 + j * NB:b * 512 + (j + 1) * NB]
                        kk = k2 if k2 <= n // 2 else n - k2
                        e.matmul(o, lhsT=Y[:, :, kk], rhs=MB1[:, k2, :], start=True, stop=False)
                        ins = e.matmul(o, lhsT=Y[:, :, n + kk], rhs=MB2[:, k2, :], start=False, stop=True)
                    return ins
                S.op("pe", mmb, ytoks + ["MB1", "MB2"], [("ps", b)])
                src = PS[:, b * 512:b * 512 + kn * NB].rearrange("p (k r i) -> p k r i", k=kn, r=2)
                dst = P.rearrange("p r (i k) -> p k r i", k=n)[:, k0:k0 + kn, :, :]
                if (k0 // kpb) % 2 == 0:
                    S.op("act", lambda e, s=src, d=dst: e.copy(out=d, in_=s), [("ps", b)], [("P", k0)])
                else:
                    S.op("dve", lambda e, s=src, d=dst: e.tensor_copy(out=d, in_=s), [("ps", b)], [("P", k0)])
            ptoks = [("P", k0) for k0 in range(0, n, kpb)]
            t0 = 0
            while t0 < ncols:
                tn = min(512, ncols - t0)
                b = nb()

                def mmc(e, t0=t0, tn=tn, b=b):
                    e.matmul(bank(b, tn), lhsT=CDt[:, 0:128], rhs=P[:, 0, idx_lo + t0:idx_lo + t0 + tn],
                             start=True, stop=False)
                    return e.matmul(bank(b, tn), lhsT=CDt[:, 128:256], rhs=P[:, 1, idx_lo + t0:idx_lo + t0 + tn],
                                    start=False, stop=True)
                S.op("pe", mmc, ptoks + ["CD"], [("ps", b)])
                S.op("act", lambda e, t0=t0, tn=tn, b=b, g=g: e.copy(
                    out=BIG[:, 4 + g, e_lo + t0:e_lo + t0 + tn], in_=bank(b, tn)),
                    [("ps", b)], blk("BIG", 4 + g, e_lo + t0, e_lo + t0 + tn))
                t0 += tn

    def mask_left():
        S.op("dve", lambda e: e.tensor_tensor(out=X[:, :, 0:3], in0=X[:, :, 0:3],
                                              in1=maskt[:, 0:48].rearrange("p (c m) -> p c m", c=8)[:, :, 0:3],
                                              op=ALU.mult),
             blks("X", range(8), 0, 3) + ["mask"], blks("X", range(8), 0, 3))

    def mask_right():
        S.op("dve", lambda e: e.tensor_tensor(out=X[:, :, W - 3:W], in0=X[:, :, W - 3:W],
                                              in1=maskt[:, 0:48].rearrange("p (c m) -> p c m", c=8)[:, :, 3:6],
                                              op=ALU.mult),
             blks("X", range(8), W - 3, W) + ["mask"], blks("X", range(8), W - 3, W))

    def mask_halo():
        mask_left()
        mask_right()

    SQ = SCR[:, 0:4096].rearrange("p (c n) -> p c n", c=8)
    SD = scr_f32(4096, 512)
    CG = [scr_f32(6144 + i_ * 1056, 528) for i_ in (0, 1)]
    CV = [scr_f32(6144 + i_ * 1056, 528) for i_ in (2, 3)]
    SG = [scr_f32(6144 + i_ * 1056, 528) for i_ in (4, 5)]
    YT = scr_f32(12288, 1024)
    OST = [scr_f32(14336, 1024)]

    RS2 = [scr_f32(5120, 512), scr_f32(16384, 512)]
    rstate = {"i": 0}

    def rms_tile(t0, tn):
        par = rstate["i"] % 2
        rstate["i"] += 1
        RSp = RS2[par]
        S.op("act", lambda e: e.activation(out=SQ[:, :, 0:tn], in_=X[:, :, t0:t0 + tn], func=AF.Square),
             blks("X", range(8), t0, t0 + tn), [("sq",)])
        b = nb()

        def mm(e, b=b):
            ins = None
            for c in range(8):
                ins = e.matmul(bank(b, tn), lhsT=onesb[:, :], rhs=SQ[:, c, 0:tn], start=(c == 0), stop=(c == 7))
            return ins
        S.op("pe", mm, [("sq",), "ones"], [("ps", b)])
        S.op("act", lambda e, b=b: e.activation(out=SD[:, 0:tn], in_=bank(b, tn), func=AF.Ln,
                                                 bias=epsT[:, 0:1], scale=float(1.0 / D)),
             [("ps", b), "eps"], [("sd",)])
        S.op("act", lambda e: e.activation(out=RSp[:, 0:tn], in_=SD[:, 0:tn], func=AF.Exp, scale=-0.5),
             [("sd",)], [("rs", par)])
        return RSp, ("rs", par)

    def norm_to_H(gi):
        S.op("pool", lambda e: e.memset(H[:, :, 0:1], 0.0), [], blks("H", range(8), 0, 1))
        S.op("pool", lambda e: e.memset(H[:, :, 2055:2056], 0.0), [], blks("H", range(8), 2055, 2056))
        for (t0, tn) in TT:
            RSp, rtok = rms_tile(t0, tn)
            for c in range(8):
                S.op("dve", lambda e, c=c, t0=t0, tn=tn, RSp=RSp: e.scalar_tensor_tensor(
                    out=H[:, c, 1 + t0:1 + t0 + tn], in0=X[:, c, t0:t0 + tn],
                    scalar=gvec[:, gi * 8 + c:gi * 8 + c + 1], in1=RSp[:, 0:tn], op0=ALU.mult, op1=ALU.mult),
                    blk("X", c, t0, t0 + tn) + [rtok, "gvec"], blk("H", c, 1 + t0, 1 + t0 + tn))

    def linear_to_X(wsrc_fn, nk, rhs_fn, rhs_toks_fn, tiles=None):
        for o in range(8):
            wb, tok = wtile([wsrc_fn(o)], nk * 128)
            for (t0, tn) in (tiles or TT):
                b = nb()

                def mm(e, wb=wb, t0=t0, tn=tn, b=b):
                    ins = None
                    for k in range(nk):
                        ins = e.matmul(bank(b, tn), lhsT=wb[:, k * 128:(k + 1) * 128], rhs=rhs_fn(k, t0, tn),
                                       start=(k == 0), stop=(k == nk - 1))
                    return ins
                S.op("pe", mm, [tok] + rhs_toks_fn(t0, tn), [("ps", b)])
                S.op("dve", lambda e, o=o, t0=t0, tn=tn, b=b: e.tensor_tensor(
                    out=X[:, o, t0:t0 + tn], in0=bank(b, tn), in1=X[:, o, t0:t0 + tn], op=ALU.add),
                    [("ps", b)] + blk("X", o, t0, t0 + tn), blk("X", o, t0, t0 + tn))

    def ffn(l):
        norm_to_H(1 + l)
        fw = fcw[:, l * 132:(l + 1) * 132].rearrange("p (a b) -> p a b", a=3)
        fb = fcb[:, l * 44:(l + 1) * 44]
        def up_pair(i):
            slot = i % 8
            if True:
                wb, tok = wtile(ffn_up_srcs(l, i), 2048)
                for ti, (h0, hn, nout) in enumerate(CT[:3] + [(1530, 526, 524)]):
                    merged = hn > 512
                    if merged:
                        bg, bv = nb2(), nb2()
                    else:
                        bg, bv = nb(), nb()

                    def mm(e, wb=wb, h0=h0, hn=hn, bg=bg, bv=bv, merged=merged):
                        ins = None
                        for (bb_, wo) in ((bg, 0), (bv, 1024)):
                            for c in range(8):
                                lt = wb[:, wo + c * 128:wo + (c + 1) * 128]
                                ins = e.matmul(bank(bb_, min(hn, 512)), lhsT=lt, rhs=H[:, c, h0:h0 + min(hn, 512)],
                                               start=(c == 0), stop=(c == 7))
                                if merged:
                                    ins = e.matmul(bank(bb_ + 1, hn - 512), lhsT=lt, rhs=H[:, c, h0 + 512:h0 + hn],
                                                   start=(c == 0), stop=(c == 7))
                        return ins
                    pall = (lambda bb_: [("ps", bb_), ("ps", bb_ + 1)]) if merged else (lambda bb_: [("ps", bb_)])
                    S.op("pe", mm, [tok] + blks("H", range(8), h0, h0 + hn), pall(bg) + pall(bv))
                    sl = uid() % 2
                    jg, jv = i, NFC + i
                    for (bb, dst, j, nm) in ((bg, CG[sl], jg, "cg"), (bv, CV[sl], jv, "cv")):
                        S.op("act", lambda e, bb=bb, dst=dst, j=j, nout=nout: e.activation(
                            out=dst[:, 0:nout], in_=PS[:, bb * 512 + 1:bb * 512 + 1 + nout], func=AF.Identity,
                            bias=fb[:, j:j + 1], scale=fw[:, 1, j:j + 1]),
                            pall(bb) + ["fcw", "fcb"], [(nm, sl)])
                        S.op("dve", lambda e, bb=bb, dst=dst, j=j, nout=nout: e.scalar_tensor_tensor(
                            out=dst[:, 0:nout], in0=PS[:, bb * 512:bb * 512 + nout], scalar=fw[:, 0, j:j + 1],
                            in1=dst[:, 0:nout], op0=ALU.mult, op1=ALU.add),
                            pall(bb) + [(nm, sl), "fcw"], [(nm, sl)])
                        S.op("dve", lambda e, bb=bb, dst=dst, j=j, nout=nout: e.scalar_tensor_tensor(
                            out=dst[:, 0:nout], in0=PS[:, bb * 512 + 2:bb * 512 + 2 + nout], scalar=fw[:, 2, j:j + 1],
                            in1=dst[:, 0:nout], op0=ALU.mult, op1=ALU.add),
                            pall(bb) + [(nm, sl), "fcw"], [(nm, sl)])
                    S.op("act", lambda e, sl=sl, nout=nout: e.activation(out=SG[sl][:, 0:nout], in_=CG[sl][:, 0:nout],
                                                                          func=AF.Silu),
                         [("cg", sl)], [("sg", sl)])
                    S.op("pool", lambda e, sl=sl, nout=nout, slot=slot, h0=h0: e.tensor_tensor(
                        out=BIG[:, slot, h0:h0 + nout], in0=SG[sl][:, 0:nout], in1=CV[sl][:, 0:nout], op=ALU.mult),
                        [("sg", sl), ("cv", sl)], blk("BIG", slot, h0, h0 + nout))
        def down(grp, tiles=None):
            nk = len(grp)
            k0 = grp[0]
            linear_to_X(
                lambda o, k0=k0, nk=nk: ffn_down_src(l, k0, nk, o),
                nk, lambda k, t0, tn, k0=k0: BIG[:, (k0 + k) % 8, t0:t0 + tn],
                lambda t0, tn, nk=nk, k0=k0: blks("BIG", [(k0 + k) % 8 for k in range(nk)], t0, t0 + tn),
                tiles=tiles)

        for i in GROUPS[0]:
            up_pair(i)
        for g in range(1, len(GROUPS)):
            for i in GROUPS[g][:2]:
                up_pair(i)
            down(GROUPS[g - 1])
            for i in GROUPS[g][2:]:
                up_pair(i)
        down(GROUPS[-1], TT[:2])
        mask_left()
        down(GROUPS[-1], TT[2:])
        mask_right()

    MM_, TMP, CC = CG, CV, SG

    def mixer_c():
        norm_to_H(0)
        cw = ccw[:, :].rearrange("p (a b) -> p a b", a=3)
        for j in range(8):
            wb, tok = wtile(mixc_srcs(j), 3072)
            for (h0, hn, nout) in CT:
                n_ = uid()
                bb_, bc, bz = (0, 1, 2)[n_ % 3], (3, 4)[n_ % 2], (5, 6, 7)[n_ % 3]

                def mm(e, wb=wb, h0=h0, hn=hn, nout=nout, bb_=bb_, bc=bc, bz=bz):
                    ins = None
                    for c in range(8):
                        e.matmul(bank(bc, hn), lhsT=wb[:, 1024 + c * 128:1024 + (c + 1) * 128],
                                 rhs=H[:, c, h0:h0 + hn], start=(c == 0), stop=(c == 7))
                    for c in range(8):
                        e.matmul(bank(bz, hn), lhsT=wb[:, 2048 + c * 128:2048 + (c + 1) * 128],
                                 rhs=H[:, c, h0:h0 + hn], start=(c == 0), stop=(c == 7))
                    for c in range(8):
                        ins = e.matmul(bank(bb_, nout), lhsT=wb[:, c * 128:(c + 1) * 128],
                                       rhs=H[:, c, h0 + 1:h0 + 1 + nout], start=(c == 0), stop=(c == 7))
                    return ins
                S.op("pe", mm, [tok] + blks("H", range(8), h0, h0 + hn), [("ps", bb_), ("ps", bc), ("ps", bz)])
                sl = n_ % 2
                S.op("act", lambda e, sl=sl, hn=hn, bc=bc: e.copy(out=TMP[sl][:, 0:hn], in_=bank(bc, hn)),
                     [("ps", bc)], [("cv", sl)])
                S.op("dve", lambda e, sl=sl, hn=hn, bz=bz: e.tensor_tensor(
                    out=MM_[sl][:, 0:hn], in0=bank(bz, hn), in1=TMP[sl][:, 0:hn], op=ALU.mult),
                    [("ps", bz), ("cv", sl)], [("cg", sl)])
                S.op("act", lambda e, sl=sl, nout=nout, j=j: e.activation(
                    out=CC[sl][:, 0:nout], in_=MM_[sl][:, 1:1 + nout], func=AF.Identity,
                    bias=ccb[:, j:j + 1], scale=cw[:, 1, j:j + 1]), [("cg", sl), "ccw", "ccb"], [("sg", sl)])
                S.op("dve", lambda e, sl=sl, nout=nout, j=j: e.scalar_tensor_tensor(
                    out=CC[sl][:, 0:nout], in0=MM_[sl][:, 0:nout], scalar=cw[:, 0, j:j + 1],
                    in1=CC[sl][:, 0:nout], op0=ALU.mult, op1=ALU.add), [("cg", sl), ("sg", sl), "ccw"], [("sg", sl)])
                S.op("dve", lambda e, sl=sl, nout=nout, j=j: e.scalar_tensor_tensor(
                    out=CC[sl][:, 0:nout], in0=MM_[sl][:, 2:2 + nout], scalar=cw[:, 2, j:j + 1],
                    in1=CC[sl][:, 0:nout], op0=ALU.mult, op1=ALU.add), [("cg", sl), ("sg", sl), "ccw"], [("sg", sl)])
                S.op("dve", lambda e, sl=sl, nout=nout, j=j, h0=h0, bb_=bb_: e.tensor_tensor(
                    out=BIG[:, j, h0:h0 + nout], in0=bank(bb_, nout), in1=CC[sl][:, 0:nout], op=ALU.mult),
                    [("ps", bb_), ("sg", sl)], blk("BIG", j, h0, h0 + nout))
        for tiles, mfn in ((TT[:2], mask_left), (TT[2:], mask_right)):
            linear_to_X(lambda o: w_out_c[:, o * 128:(o + 1) * 128].rearrange("(k p) o -> p k o", p=128),
                        8, lambda k, t0, tn: BIG[:, k, t0:t0 + tn],
                        lambda t0, tn: blks("BIG", range(8), t0, t0 + tn), tiles=tiles)
            mfn()

    def final_out(slab):
        S.barrier()
        wprefetch_extra()
        YTs = [YT, scr_f32(6144, 1024)]
        OSTs = [OST[0], scr_f32(8192, 1024)]
        rs_of = {}

        def stage1(i):
            ti, qq = divmod(i, 4)
            t0 = HALO + ti * 512
            if i == 0:
                rs_of[0] = rms_tile(t0, 512)
            if qq == 1 and ti + 1 < 4:
                rs_of[ti + 1] = rms_tile(t0 + 512, 512)
            RSp, rtok = rs_of[ti]
            c0 = t0 + qq * 128
            yt = YTs[i % 2]
            for c in range(8):
                S.op("dve", lambda e, c=c, c0=c0, qq=qq, RSp=RSp, yt=yt: e.scalar_tensor_tensor(
                    out=yt[:, c * 128:(c + 1) * 128], in0=X[:, c, c0:c0 + 128],
                    scalar=gvec[:, 24 + c:25 + c], in1=RSp[:, qq * 128:(qq + 1) * 128],
                    op0=ALU.mult, op1=ALU.mult),
                    blk("X", c, c0, c0 + 128) + [rtok, "gvec"], [("yt", i % 2, c)])

        def stage2(i):
            yt, ost = YTs[i % 2], OSTs[i % 2]
            b2 = nb2()

            def tr(e, b2=b2, yt=yt):
                ins = None
                for c in range(8):
                    ins = e.transpose(out=PS[:, b2 * 512 + c * 128:b2 * 512 + (c + 1) * 128],
                                      in_=yt[:, c * 128:(c + 1) * 128], identity=identf[:, :])
                return ins
            S.op("pe", tr, [("yt", i % 2, c) for c in range(8)] + ["identf"], [("ps", b2), ("ps", b2 + 1)])
            S.op("act", lambda e, b2=b2, ost=ost: e.copy(out=ost, in_=PS[:, b2 * 512:b2 * 512 + 1024]),
                 [("ps", b2), ("ps", b2 + 1)], [("ost", i % 2)])
            r0 = i * 128
            S.dma("pool", lambda e, r0=r0, ost=ost: e.dma_start(out=yout[slab][r0:r0 + 128, :], in_=ost),
                  [("ost", i % 2)], [("yout", slab, r0)], ("ost", i % 2))

        for i in range(17):
            if i < 16:
                stage1(i)
            if i >= 1:
                stage2(i - 1)
        S.barrier()

    for slab in ("s", "p"):
        S.barrier()
        S.dma("sp", lambda e, slab=slab: e.dma_start(out=maskt[:, :], in_=maskd[slab].rearrange("p c m -> p (c m)")),
              [], ["mask"], ("c", "mask"))
        f_pass(slab, 128 if slab == "s" else 16)
        S.barrier()
        dft(slab)
        S.barrier()
        front(slab)
        S.barrier()
        for tiles, mfn in ((TT[:2], mask_left), (TT[2:], mask_right)):
            linear_to_X(lambda o: w_out_ab[:, o * 128:(o + 1) * 128].rearrange("(k p) o -> p k o", p=128),
                        8, lambda k, t0, tn: BIG[:, k, t0:t0 + tn],
                        lambda t0, tn: blks("BIG", range(8), t0, t0 + tn), tiles=tiles)
            mfn()
        ffn(0)
        mixer_c()
        ffn(1)
        final_out(slab)
    S.emit(nc)
    return nc


def _tables():
    t = {}
    t["identf"] = np.eye(128, dtype=np.float32)
    t["identb"] = np.eye(128, dtype=np.float32).astype(NPBF)
    c = np.arange(128)
    ang = 2 * np.pi * np.outer(c, c) / 128.0
    cd = np.stack([np.cos(ang), np.sin(ang)], axis=1) / np.sqrt(128.0)
    t["CD"] = cd.astype(np.float32).astype(NPBF)
    for slab, n in (("p", 16), ("s", 128)):
        s2 = np.arange(n)
        a = 2 * np.pi * np.outer(s2, s2) / n
        t["WA_" + slab] = np.concatenate([np.cos(a), -np.sin(a)], axis=1).astype(np.float32).astype(NPBF)
    return t


def _mt_prompt():
    s1 = np.arange(128, dtype=np.float64)[:, None, None]
    s2 = np.arange(16, dtype=np.float64)[None, :, None]
    out = np.zeros((4, 2, 128, 16, 512), dtype=np.float32)
    sc = 1.0 / np.sqrt(float(S_P))
    for kt in range(4):
        k = (kt * 512 + np.arange(512, dtype=np.float64))[None, None, :]
        ang = 2 * np.pi * np.mod((s2 * 128 + s1) * k, S_P) / S_P
        out[kt, 0] = np.cos(ang) * sc
        out[kt, 1] = -np.sin(ang) * sc
    return out.reshape(4, 2, 128, 16 * 512).astype(NPBF)


def _mb(S_len, n, k1_list):
    s1 = np.arange(128, dtype=np.float64)[:, None, None]
    k2 = np.arange(n, dtype=np.float64)[None, :, None]
    k1 = np.asarray(k1_list, dtype=np.float64)[None, None, :]
    k = np.mod(n * k1 + k2, S_len)
    ang = 2 * np.pi * np.mod(s1 * k, S_len) / S_len
    sc = 1.0 / np.sqrt(S_len)
    mr = np.cos(ang) * sc
    mi = -np.sin(ang) * sc
    mb1 = np.concatenate([mr, mi], axis=2)
    mb2 = np.concatenate([-mi, mr], axis=2)
    if n == 128:
        mb2[:, n // 2 + 1:, :] *= -1.0
    return mb1.astype(np.float32).astype(NPBF), mb2.astype(np.float32).astype(NPBF)


_CACHE = {}


def kernel(x_prompt, x_sample, norm_mix, w_in_ab, sgu_gain, w_s, b_s, w_out_ab,
           w_in_c, conv_w_c, conv_b_c, w_out_c, norm_ffn, w_up, ffn_conv_w, ffn_conv_b, w_down, final_norm):
    f32 = np.float32
    A = lambda a: np.ascontiguousarray(np.asarray(a, dtype=f32))
    x_prompt, x_sample = A(x_prompt), A(x_sample)
    if "nc" not in _CACHE:
        _CACHE["nc"] = build_program()
        _CACHE["tab"] = _tables()
    nc = _CACHE["nc"]
    tab = _CACHE["tab"]
    shared = {
        "w_in_ab": A(w_in_ab)[0], "w_out_ab": A(w_out_ab)[0], "w_in_c": A(w_in_c)[0], "w_out_c": A(w_out_c)[0],
        "w_up": A(w_up), "w_down": A(w_down),
        "g0bc": np.ascontiguousarray(np.broadcast_to(A(norm_mix)[0][None, :], (128, D))),
        "gainbc": np.ascontiguousarray(np.broadcast_to(A(sgu_gain)[0].reshape(1, 512), (128, 512))),
        "bst": np.ascontiguousarray(A(b_s)[0].T),
        "wsT": np.ascontiguousarray(A(w_s)[0].transpose(2, 0, 1)),
        "gvec": np.ascontiguousarray(np.stack([A(norm_mix)[1], A(norm_ffn)[0], A(norm_ffn)[1], A(final_norm)], 0)
                                     .reshape(4, 8, 128).transpose(2, 0, 1)),
        "ccw": np.ascontiguousarray(A(conv_w_c)[0].reshape(3, 8, 128).transpose(2, 0, 1)),
        "ccb": np.ascontiguousarray(A(conv_b_c)[0].reshape(8, 128).T),
        "fcw": np.ascontiguousarray(A(ffn_conv_w).reshape(2, 3, 44, 128).transpose(3, 0, 1, 2)),
        "fcb": np.ascontiguousarray(A(ffn_conv_b).reshape(2, 44, 128).transpose(2, 0, 1)),
        "identf": tab["identf"], "identb": tab["identb"], "CD": tab["CD"],
        "WA_p": tab["WA_p"], "WA_s": tab["WA_s"],
        "xa_s": x_sample[0],
    }
    mb1p, mb2p = _mb(S_P, 16, list(range(128)))
    shared["MB1_p"], shared["MB2_p"] = mb1p, mb2p
    if "mt" not in _CACHE:
        _CACHE["mt"] = _mt_prompt()
    shared["MT_p"] = _CACHE["mt"]
    in_maps = []
    for j in range(NCORE):
        m = dict(shared)
        m["xa_p"] = x_prompt[j]
        xe_p = np.zeros((18 * 128, D), f32)
        xe_p[128:128 + S_P] = x_prompt[j]
        m["xe_p"] = xe_p
        xe_s = np.zeros((18 * 128, D), f32)
        lo = 2048 * j - 128
        hi = lo + 18 * 128
        slo, shi = max(lo, 0), min(hi, S_S)
        xe_s[slo - lo:shi - lo] = x_sample[0, slo:shi]
        m["xe_s"] = xe_s
        m["mask_p"] = np.zeros((128, 8, 6), f32)
        ms = np.ones((128, 8, 6), f32)
        if j == 0:
            ms[:, :, 0:3] = 0
        if j == NCORE - 1:
            ms[:, :, 3:6] = 0
        m["mask_s"] = ms
        mb1, mb2 = _mb(S_S, 128, [16 * j - 1 + i for i in range(18)])
        m["MB1_s"], m["MB2_s"] = mb1, mb2
        in_maps.append(m)
    res = run_bass_kernel_spmd(nc, in_maps, core_ids=list(range(NCORE)))
    yp = np.stack([np.asarray(res.results[j]["y_p"], dtype=f32) for j in range(NCORE)], 0)
    ys = np.concatenate([np.asarray(res.results[j]["y_s"], dtype=f32) for j in range(NCORE)], 0)[None]
    return yp, ys
```
